# Optimizing a Trainium2 kernel written in Bass

```python
import jax, jax.numpy as jnp
from jax import lax
import numpy as np

D_MODEL = 1024
BATCH = 4
SEQ = 8192
DEPTH = 2

N_MIXERS = 2
N_META = 16
EXPAND = 2
E_POOL = EXPAND * D_MODEL
POOL_WINDOWS = (2, 4, 8, 16)
N_POOL_GROUPS = len(POOL_WINDOWS)
POOL_GROUP_DIM = E_POOL // N_POOL_GROUPS
E_HGRN = EXPAND * D_MODEL
HGRN_HEAD_DIM = 128
N_HGRN_HEADS = E_HGRN // HGRN_HEAD_DIM
CHUNK = 64
PAD_FRONT = (-N_META) % CHUNK
N_POOL_LAYERS = (DEPTH + 1) // 2
N_HGRN_LAYERS = DEPTH // 2
EPS = 1e-6

kernel_name = "hybrid_pool_hgrn2_meta"


def rmsnorm(x, w):
    x32 = x.astype(jnp.float32)
    y = x32 * lax.rsqrt(jnp.mean(x32 * x32, axis=-1, keepdims=True) + EPS)
    return (y * w.astype(jnp.float32)).astype(x.dtype)


def pool_mixer(h, w_in, w_grp, scale, w_out):
    b_, L, _ = h.shape
    v, gate = jnp.split(h @ w_in, 2, axis=-1)
    v32 = v.astype(jnp.float32)
    cs = jnp.cumsum(v32, axis=1)
    t = jnp.arange(L)
    groups = []
    for g, w in enumerate(POOL_WINDOWS):
        sl = slice(g * POOL_GROUP_DIM, (g + 1) * POOL_GROUP_DIM)
        c = cs[..., sl]
        c_shift = jnp.pad(c, ((0, 0), (w, 0), (0, 0)))[:, :L]
        cnt = jnp.minimum(t + 1, w).astype(jnp.float32)[None, :, None]
        groups.append((c - c_shift) / cnt - v32[..., sl])
    u = jnp.stack(groups, axis=2).astype(h.dtype)
    u = jnp.einsum('blgc,gcd->blgd', u, w_grp).reshape(b_, L, E_POOL) * scale
    return (u * jax.nn.silu(gate)) @ w_out


def _hgrn_chunk_step(S, xs):
    q, k, v, g = xs
    c = q.shape[2]
    causal = jnp.tril(jnp.ones((c, c), dtype=bool))
    bcum = jnp.cumsum(g, axis=2)
    rel = bcum[:, :, :, None, :] - bcum[:, :, None, :, :]
    decay = jnp.exp(jnp.where(causal[:, :, None], rel, -jnp.inf))
    scores = jnp.einsum('bhtk,bhsk,bhtsk->bhts', q, k, decay)
    o = jnp.einsum('bhts,bhsv->bhtv', scores, v) \
        + jnp.einsum('bhtk,bhkv->bhtv', q * jnp.exp(bcum), S)
    b_end = bcum[:, :, -1, :]
    S_new = jnp.exp(b_end)[..., None] * S \
        + jnp.einsum('bhsk,bhsv->bhkv', k * jnp.exp(b_end[:, :, None, :] - bcum), v)
    return S_new, o


def hgrn2_mixer(h, w_in, lb, o_norm, w_out):
    b_, L, _ = h.shape
    q, fp, i_in, gate = jnp.split(h @ w_in, 4, axis=-1)
    q = jax.nn.silu(q)
    fp32 = fp.astype(jnp.float32)
    lb32 = lb.astype(jnp.float32)
    f = lb32 + (1.0 - lb32) * jax.nn.sigmoid(fp32)
    log_f = jnp.log(f)
    k = (1.0 - lb32) * jax.nn.sigmoid(-fp32)

    def to_chunks(a):
        a = jnp.pad(a.astype(jnp.float32), ((0, 0), (PAD_FRONT, 0), (0, 0)))
        n = a.shape[1] // CHUNK
        return a.reshape(b_, n, CHUNK, N_HGRN_HEADS, HGRN_HEAD_DIM).transpose(1, 0, 3, 2, 4)

    xs = (to_chunks(q), to_chunks(k), to_chunks(i_in), to_chunks(log_f))
    S0 = jnp.zeros((b_, N_HGRN_HEADS, HGRN_HEAD_DIM, HGRN_HEAD_DIM), jnp.float32)
    _, o = lax.scan(_hgrn_chunk_step, S0, xs)
    n = o.shape[0]
    o = o.transpose(1, 0, 3, 2, 4).reshape(b_, n * CHUNK, N_HGRN_HEADS, HGRN_HEAD_DIM)[:, PAD_FRONT:]
    o = o * lax.rsqrt(jnp.mean(o * o, axis=-1, keepdims=True) + EPS)
    o = o.reshape(b_, L, E_HGRN) * o_norm.astype(jnp.float32)
    return (o.astype(h.dtype) * jax.nn.silu(gate)) @ w_out


def setup_inputs(seed: int = 0) -> dict:
    key = jax.random.key(seed)
    ks = jax.random.split(key, 12)
    nrm = jax.random.normal
    f32 = jnp.float32
    return {
        "x": nrm(ks[0], (BATCH, SEQ, D_MODEL), f32),
        "meta_tokens": nrm(ks[1], (N_META, D_MODEL), f32),
        "norm_w": 1.0 + 0.05 * nrm(ks[2], (DEPTH, D_MODEL), f32),
        "pool_w_in": nrm(ks[3], (N_POOL_LAYERS, D_MODEL, 2 * E_POOL), f32) * D_MODEL ** -0.5,
        "pool_w_grp": nrm(ks[4], (N_POOL_LAYERS, N_POOL_GROUPS, POOL_GROUP_DIM, POOL_GROUP_DIM), f32) * POOL_GROUP_DIM ** -0.5,
        "pool_scale": 1.0 + 0.1 * nrm(ks[5], (N_POOL_LAYERS, E_POOL), f32),
        "pool_w_out": nrm(ks[6], (N_POOL_LAYERS, E_POOL, D_MODEL), f32) * E_POOL ** -0.5,
        "hgrn_w_in": nrm(ks[7], (N_HGRN_LAYERS, D_MODEL, 4 * E_HGRN), f32) * D_MODEL ** -0.5,
        "hgrn_lb_logits": 1.0 + 0.5 * nrm(ks[8], (DEPTH, E_HGRN), f32),
        "hgrn_o_norm": 1.0 + 0.05 * nrm(ks[9], (N_HGRN_LAYERS, E_HGRN), f32),
        "hgrn_w_out": nrm(ks[10], (N_HGRN_LAYERS, E_HGRN, D_MODEL), f32) * E_HGRN ** -0.5,
        "final_norm_w": 1.0 + 0.05 * nrm(ks[11], (D_MODEL,), f32),
    }


def reference(x, meta_tokens, norm_w, pool_w_in, pool_w_grp, pool_scale, pool_w_out,
              hgrn_w_in, hgrn_lb_logits, hgrn_o_norm, hgrn_w_out, final_norm_w):
    b_ = x.shape[0]
    meta = jnp.broadcast_to(meta_tokens.astype(x.dtype)[None], (b_, N_META, x.shape[-1]))
    h = jnp.concatenate([meta, x], axis=1)
    p = jax.nn.softmax(hgrn_lb_logits.astype(jnp.float32), axis=0)
    lb_all = jnp.cumsum(p, axis=0) - p[0]
    for i in range(DEPTH):
        hn = rmsnorm(h, norm_w[i])
        j = i // N_MIXERS
        if i % N_MIXERS == 0:
            y = pool_mixer(hn, pool_w_in[j], pool_w_grp[j], pool_scale[j], pool_w_out[j])
        else:
            y = hgrn2_mixer(hn, hgrn_w_in[j], lb_all[i], hgrn_o_norm[j], hgrn_w_out[j])
        h = h + y
    return rmsnorm(h[:, N_META:], final_norm_w)
```

```python
from contextlib import ExitStack
import numpy as np
import ml_dtypes
import concourse.bass as bass
import concourse.mybir as mybir
from concourse.bass_utils import run_bass_kernel_spmd

F32 = mybir.dt.float32
BF16 = mybir.dt.bfloat16
AF = mybir.ActivationFunctionType
ALU = mybir.AluOpType

D = 1024
NMETA = 16
SEQ = 8192
HALF = SEQ // 2
EPS = 1e-6
LTOK = NMETA + HALF
LFULL = NMETA + SEQ

SCRATCH = 4096

ENGS = ("pe", "act", "dve", "pool", "sp")


class Prog:
    def __init__(self, nc):
        self.nc = nc
        self.ops = []
        self.phase = 0

    def barrier(self):
        self.phase += 1

    def op(self, eng, fn, reads=(), writes=(), dma_slot=None, npieces=1, inc=16, exempt=False):
        assert eng in ENGS
        self.ops.append(dict(eng=eng, fn=fn, reads=tuple(reads), writes=tuple(writes),
                             dma_slot=dma_slot, npieces=npieces, inc=inc, phase=self.phase, exempt=exempt))

    def finish(self, es):
        nc = self.nc
        ops = self.ops
        last_writer = {}
        readers = {}
        prev_last, cur_last, cur_phase = {}, {}, 0
        for i, o in enumerate(ops):
            if o["phase"] != cur_phase:
                prev_last.update(cur_last)
                cur_last, cur_phase = {}, o["phase"]
            deps = set(prev_last.values())
            if not o["exempt"]:
                cur_last[("dma", o["dma_slot"]) if o["dma_slot"] is not None else o["eng"]] = i
            for k in o["reads"]:
                if k in last_writer:
                    deps.add(last_writer[k])
            for k in o["writes"]:
                if k in last_writer:
                    deps.add(last_writer[k])
                for r in readers.get(k, ()):
                    deps.add(r)
            deps.discard(i)
            o["deps"] = deps
            for k in o["reads"]:
                readers.setdefault(k, []).append(i)
            for k in o["writes"]:
                last_writer[k] = i
                readers[k] = []

        def chan(o):
            return ("dma", o["dma_slot"]) if o["dma_slot"] is not None else o["eng"]

        for o in ops:
            o["marked"] = False
        for i, o in enumerate(ops):
            best = {}
            for d in o["deps"]:
                so = ops[d]
                c = chan(so)
                if so["dma_slot"] is None and so["eng"] == "pe" and o["eng"] == "pe" and o["dma_slot"] is None:
                    continue
                if c not in best or best[c] < d:
                    best[c] = d
            o["need"] = best
            for d in best.values():
                ops[d]["marked"] = True
        cnt = {}
        for o in ops:
            c = chan(o)
            if o["dma_slot"] is not None:
                cnt[c] = cnt.get(c, 0) + o["inc"] * o["npieces"]
                o["count"] = cnt[c]
            elif o["marked"]:
                cnt[c] = cnt.get(c, 0) + 1
                o["count"] = cnt[c]
        sems = {}
        for c in cnt:
            nm = "s_" + (c if isinstance(c, str) else "d_" + str(c[1]))
            sems[c] = es.enter_context(nc.semaphore(nm))
        engobj = {"pe": "tensor", "act": "scalar", "dve": "vector", "pool": "gpsimd", "sp": "sync"}
        with nc.Block() as block:
            for e in ENGS:
                my = [o for o in ops if o["eng"] == e]
                if not my:
                    continue

                def body(eng, my=my, e=e):
                    waited = {}
                    for o in my:
                        for c, d in sorted(o["need"].items(), key=lambda kv: str(kv[0])):
                            v = ops[d]["count"]
                            if waited.get(c, 0) < v:
                                eng.wait_ge(sems[c], v)
                                waited[c] = v
                        if o["dma_slot"] is not None:
                            insts = o["fn"](eng)
                            if not isinstance(insts, (list, tuple)):
                                insts = [insts]
                            assert len(insts) == o["npieces"], (len(insts), o["npieces"])
                            for ins in insts:
                                ins.then_inc(sems[("dma", o["dma_slot"])], o["inc"])
                        else:
                            ins = o["fn"](eng)
                            if o["marked"]:
                                ins.then_inc(sems[e], 1)
                    final = {}
                    for o in my:
                        if o["dma_slot"] is not None:
                            c = ("dma", o["dma_slot"])
                            final[c] = max(final.get(c, 0), o["count"])
                    for c, v in final.items():
                        if waited.get(c, 0) < v:
                            eng.wait_ge(sems[c], v)

                getattr(block, engobj[e])(body)


class SB:
    def __init__(self, nc, pref, base=SCRATCH, limit=192 * 1024):
        self.nc, self.pref, self.off, self.limit = nc, pref, base, limit

    def t(self, name, shape, dt):
        n = 1
        for s in shape[1:]:
            n *= s
        nbytes = n * mybir.dt.size(dt)
        nbytes = (nbytes + 63) // 64 * 64
        h = self.nc.alloc_sbuf_tensor_at(self.pref + name, list(shape), dt, offset=self.off)
        self.off += nbytes
        assert self.off <= self.limit, (self.pref, name, self.off)
        return h


class Rot:
    def __init__(self, items):
        self.items, self.i = list(items), 0

    def next(self):
        x = self.items[self.i % len(self.items)]
        self.i += 1
        return x


def mm_group(P, out_ap, pairs, reads, writes):
    def fn(e, out_ap=out_ap, pairs=pairs):
        n = len(pairs)
        ins = None
        for i, (l, r) in enumerate(pairs):
            ins = e.matmul(out_ap, l, r, start=(i == 0), stop=(i == n - 1))
        return ins
    P.op("pe", fn, reads=reads, writes=writes)


def load_weight_bf16(P, eng, dst, src, key, slot):
    kt = dst.shape[1]
    def fn(e):
        return [e.dma_start(out=dst[:, k, :], in_=src[k * 128:(k + 1) * 128, :]) for k in range(kt)]
    P.op(eng, fn, writes=[key], dma_slot=slot, npieces=kt)


def load_weight_cols(P, eng, dst, src, c0, c1, key, slot):
    kt = dst.shape[1]
    def fn(e):
        return [e.dma_start(out=dst[:, k, c0:c1], in_=src[k * 128:(k + 1) * 128, c0:c1]) for k in range(kt)]
    P.op(eng, fn, writes=[key], dma_slot=slot, npieces=kt)


def make_ident(P, sb, name):
    ident = sb.t(name, [128, 128], BF16)
    key = sb.pref + name
    P.op("pool", lambda e: e.memset(ident[:], 1.0), writes=[key])
    P.op("pool", lambda e: e.affine_select(ident[:], ident[:], [[-1, 128]], ALU.is_equal, 0.0,
                                           base=0, channel_multiplier=1), reads=[key], writes=[key])
    return ident


def rms_rstd(P, ss_ap, rs_ap, n, keys_r, keys_w):
    P.op("act", lambda e: e.activation(rs_ap, ss_ap, AF.Sqrt, bias=float(n) * EPS), reads=keys_r, writes=keys_w)
    P.op("dve", lambda e: e.reciprocal(rs_ap, rs_ap), reads=keys_w, writes=keys_w)


def build_p1(nc, P, ps, a, ntiles, T=256, pref="p1", after_tile=None):
    sb = SB(nc, pref)
    K = lambda *k: (pref,) + k
    NB = T // 128
    W_in = sb.t("W_in", [128, 8, 4096], BF16)
    W_grp = sb.t("W_grp", [128, 16, 512], BF16)
    W_out = sb.t("W_out", [128, 16, 1024], BF16)
    wt0 = sb.t("wt0", [128, D], F32)
    wt1 = sb.t("wt1", [128, D], F32)
    scale = sb.t("scale", [128, 16], F32)
    ident = make_ident(P, sb, "ident")
    xt = [sb.t(f"xt{i}", [128, NB, D], F32) for i in range(2)]
    hn = [sb.t(f"hn{i}", [128, D], BF16) for i in range(2)]
    hnT = [sb.t(f"hnT{i}", [128, 8, T], BF16) for i in range(2)]
    NV = 4
    vb = [sb.t(f"vb{i}", [128, 16 + T], F32) for i in range(NV)]
    sA = [sb.t(f"sA{i}", [128, 16 + T], F32) for i in range(NV)]
    sB = [sb.t(f"sB{i}", [128, 16 + T], F32) for i in range(NV)]
    halo = sb.t("halo", [128, 16, 16], F32)
    u = [sb.t(f"u{i}", [128, 4, T], BF16) for i in range(2)]
    sg = [sb.t(f"sg{i}", [128, 4, T], BF16) for i in range(2)]
    m0 = sb.t("m0", [128, 16, T], BF16)
    hn1 = [sb.t("hn1_0", [128, NB, D], BF16)] * 2
    junk = sb.t("junk", [128, D], BF16)
    ss = sb.t("ss", [128, 8], F32)
    rs = sb.t("rs", [128, 8], F32)
    rc = sb.t("rc", [128, 4, 16], F32)

    P.op("sp", lambda e: e.dma_start(out=wt0[:], in_=a["nw0"].partition_broadcast(128)), writes=[K("wt0")], dma_slot=pref + "c0")
    P.op("sp", lambda e: e.dma_start(out=wt1[:], in_=a["nw1"].partition_broadcast(128)), writes=[K("wt1")], dma_slot=pref + "c1")
    P.op("sp", lambda e: e.dma_start(out=scale[:], in_=a["scale"].rearrange("(j p) -> p j", p=128),
                                     allow_slow_non_contiguous=True), writes=[K("scale")], dma_slot=pref + "c2")
    sqD = float(np.sqrt(float(D)))
    P.op("dve", lambda e: e.tensor_scalar(wt0[:], wt0[:], sqD, None, ALU.mult), reads=[K("wt0")], writes=[K("wt0")])
    P.op("dve", lambda e: e.tensor_scalar(wt1[:], wt1[:], sqD, None, ALU.mult), reads=[K("wt1")], writes=[K("wt1")])
    P.op("pool", lambda e: e.memset(halo[:], 0.0), writes=[K("halo", j) for j in range(16)])
    load_weight_bf16(P, "pool", W_in, a["w_in"], K("W_in"), pref + "w0")
    P.op("pool", lambda e: [e.dma_start(out=W_grp[:, g * 4 + kk, :], in_=a["w_grp"][g, kk * 128:(kk + 1) * 128, :])
                            for g in range(4) for kk in range(4)], writes=[K("W_grp")], dma_slot=pref + "w1", npieces=16)
    load_weight_bf16(P, "pool", W_out, a["w_out"], K("W_out"), pref + "w2")

    def rcfn(e):
        ins = None
        for g, w in enumerate((2, 4, 8, 16)):
            ins = e.memset(rc[:, g, :], 1.0 / w)
        return ins
    P.op("pool", rcfn, writes=[K("rc")])

    def rcfn2(e):
        ins = None
        for g, w in enumerate((2, 4, 8, 16)):
            for t in range(w - 1):
                ins = e.memset(rc[:, g, t:t + 1], 1.0 / (t + 1))
        return ins
    P.op("pool", rcfn2, reads=[K("rc")], writes=[K("rc")])

    rot_T = Rot([0])
    rot_proj = Rot([1, 2, 3])
    rot_grp = Rot([4, 5])
    rot_out = Rot([6, 7])
    rot_v = Rot(list(range(NV)))
    rot_hn = Rot([0, 1])

    tiles = [(0, 16)] + [(16 + T * i, T) for i in range(ntiles)]

    def load_x(ti):
        r0, nt = tiles[ti]
        s = ti % 2
        if nt == 16:
            P.op("sp", lambda e: e.dma_start(out=xt[s][0:16, 0, :], in_=a["xin"][r0:r0 + 16, :]),
                 writes=[K("xt", s)], dma_slot=pref + f"x{s}")
        else:
            P.op("sp", lambda e: e.dma_start(out=xt[s][:, :, :], in_=a["xin"][r0:r0 + nt, :].rearrange("(b p) d -> p b d", p=128)),
                 writes=[K("xt", s)], dma_slot=pref + f"x{s}")

    hs_of = {}

    def norm_part(ti):
        r0, nt = tiles[ti]
        s = ti % 2
        nb = max(1, nt // 128)
        bl = min(nt, 128)
        for b in range(nb):
            P.op("act", lambda e, b=b: e.activation(junk[0:bl, :], xt[s][0:bl, b, :], AF.Square, accum_out=ss[0:bl, b:b + 1]),
                 reads=[K("xt", s)], writes=[K("junk"), K("ss")])
        rms_rstd(P, ss[0:bl, 0:nb], rs[0:bl, 0:nb], D, [K("ss")], [K("rs")])
        hs_of[ti] = []
        for b in range(nb):
            hs = rot_hn.next()
            hs_of[ti].append(hs)
            P.op("dve", lambda e, b=b, hs=hs: e.scalar_tensor_tensor(hn[hs][0:bl, :], xt[s][0:bl, b, :], rs[0:bl, b:b + 1], wt0[0:bl, :],
                                                                     ALU.mult, ALU.mult),
                 reads=[K("xt", s), K("rs"), K("wt0")], writes=[K("hn", hs)])

    def tr_part(ti):
        r0, nt = tiles[ti]
        s = ti % 2
        nb = max(1, nt // 128)
        bl = min(nt, 128)
        for b in range(nb):
            hs = hs_of[ti][b]
            bank = rot_T.next()
            pst = ps[bank].bitcast(BF16)

            def tfn(e, hs=hs, pst=pst):
                ins = None
                for k in range(8):
                    ins = e.transpose(pst[:, k * 128:k * 128 + bl], hn[hs][0:bl, k * 128:(k + 1) * 128], ident[0:bl, 0:bl])
                return ins
            P.op("pe", tfn, reads=[K("hn", hs), pref + "ident"], writes=[("ps", bank)])
            P.op("act", lambda e, b=b, pst=pst: e.activation(hnT[s][:, :, b * 128:b * 128 + bl],
                                                             pst[:, :].rearrange("p (k t) -> p k t", t=128)[:, :, 0:bl], AF.Copy),
                 reads=[("ps", bank)], writes=[K("hnT", s)])

    def inproj(ti, g):
        r0, nt = tiles[ti]
        s = ti % 2
        gp = g % 2
        first = (nt == 16)
        wwin = (2, 4, 8, 16)[g]
        E = 16 + nt
        for pair in range(2):
            chains = []
            for jj in (2 * pair, 2 * pair + 1):
                j = g * 4 + jj
                bank = rot_proj.next()
                mm_group(P, ps[bank][:, 0:nt], [(W_in[:, k, j * 128:(j + 1) * 128], hnT[s][:, k, 0:nt]) for k in range(8)],
                         reads=[K("W_in"), K("hnT", s)], writes=[("ps", bank)])
                vs = rot_v.next()
                V, A, B = vb[vs], sA[vs], sB[vs]
                kV, kA, kB = K("vb", vs), K("sA", vs), K("sB", vs)
                kH = K("vbh", vs)
                P.op("pool", lambda e, V=V, j=j: e.tensor_copy(V[:, 0:16], halo[:, j, :]), reads=[K("halo", j)], writes=[kH])
                P.op("act", lambda e, V=V, bank=bank: e.activation(V[:, 16:16 + nt], ps[bank][:, 0:nt], AF.Copy),
                     reads=[("ps", bank)], writes=[kV])
                P.op("pool", lambda e, V=V, j=j: e.tensor_copy(halo[:, j, :], V[:, nt:nt + 16]), reads=[kV], writes=[K("halo", j)])
                ch = []
                ch.append((lambda e, V=V, A=A: e.tensor_tensor(A[:, 2:E], V[:, 2:E], V[:, 1:E - 1], ALU.add), [kV, kH], [kA]))
                src, ksrc = A, kA
                if g >= 1:
                    ch.append((lambda e, A=A, B=B: e.tensor_tensor(B[:, 4:E], A[:, 4:E], A[:, 2:E - 2], ALU.add), [kA], [kB]))
                    src, ksrc = B, kB
                if g >= 2:
                    ch.append((lambda e, A=A, B=B: e.tensor_tensor(A[:, 8:E], B[:, 8:E], B[:, 4:E - 4], ALU.add), [kB], [kA]))
                    src, ksrc = A, kA
                if g >= 3:
                    ch.append((lambda e, A=A, B=B: e.tensor_tensor(B[:, 16:E], A[:, 16:E], A[:, 8:E - 8], ALU.add), [kA], [kB]))
                    src, ksrc = B, kB
                if first:
                    ch.append((lambda e, src=src: e.tensor_tensor(src[:, 16:E], src[:, 16:E], rc[:, g, :], ALU.mult), [ksrc, K("rc")], [ksrc]))
                    ch.append((lambda e, src=src, V=V, jj=jj: e.tensor_tensor(u[gp][:, jj, 0:nt], src[:, 16:E], V[:, 16:E], ALU.subtract),
                               [ksrc, kV], [K("u", gp)]))
                else:
                    ch.append((lambda e, src=src, V=V, jj=jj: e.scalar_tensor_tensor(u[gp][:, jj, 0:nt], src[:, 16:E], 1.0 / wwin, V[:, 16:E],
                                                                                   ALU.mult, ALU.subtract), [ksrc, kV], [K("u", gp)]))
                chains.append(ch)
            for step in range(len(chains[0])):
                for ch in chains:
                    fn, rd, wr = ch[step]
                    P.op("dve", fn, reads=rd, writes=wr)
        for jj in range(4):
            j = g * 4 + jj
            bank = rot_proj.next()
            mm_group(P, ps[bank][:, 0:nt], [(W_in[:, k, 2048 + j * 128:2048 + (j + 1) * 128], hnT[s][:, k, 0:nt]) for k in range(8)],
                     reads=[K("W_in"), K("hnT", s)], writes=[("ps", bank)])
            P.op("act", lambda e, bank=bank, jj=jj: e.activation(sg[gp][:, jj, 0:nt], ps[bank][:, 0:nt], AF.Silu),
                 reads=[("ps", bank)], writes=[K("sg", gp)])

    def grp(ti, g):
        r0, nt = tiles[ti]
        gp = g % 2
        for jj in range(4):
            j = g * 4 + jj
            bank = rot_grp.next()
            mm_group(P, ps[bank][:, 0:nt], [(W_grp[:, g * 4 + kk, jj * 128:(jj + 1) * 128], u[gp][:, kk, 0:nt]) for kk in range(4)],
                     reads=[K("W_grp"), K("u", gp)], writes=[("ps", bank)])
            P.op("dve", lambda e, bank=bank, j=j, jj=jj: e.scalar_tensor_tensor(m0[:, j, 0:nt], ps[bank][:, 0:nt], scale[:, j:j + 1],
                                                                              sg[gp][:, jj, 0:nt], ALU.mult, ALU.mult),
                 reads=[("ps", bank), K("scale"), K("sg", gp)], writes=[K("m0")])

    def outproj(ti):
        r0, nt = tiles[ti]
        s = ti % 2
        nb = max(1, nt // 128)
        bl = min(nt, 128)
        for b in range(nb):
            for hf in range(2):
                bank = rot_out.next()
                mm_group(P, ps[bank][0:bl, :], [(m0[:, k, b * 128:b * 128 + bl], W_out[:, k, hf * 512:(hf + 1) * 512]) for k in range(16)],
                         reads=[K("m0"), K("W_out")], writes=[("ps", bank)])
                P.op("dve", lambda e, b=b, hf=hf, bank=bank: e.tensor_tensor(xt[s][0:bl, b, hf * 512:(hf + 1) * 512],
                                                                           xt[s][0:bl, b, hf * 512:(hf + 1) * 512], ps[bank][0:bl, :], ALU.add),
                     reads=[("ps", bank), K("xt", s)], writes=[K("xt", s)])
            P.op("act", lambda e, b=b: e.activation(junk[0:bl, :], xt[s][0:bl, b, :], AF.Square, accum_out=ss[0:bl, 4 + b:5 + b]),
                 reads=[K("xt", s)], writes=[K("junk"), K("ss1")])
        rms_rstd(P, ss[0:bl, 4:4 + nb], rs[0:bl, 4:4 + nb], D, [K("ss1")], [K("rs1")])
        for b in range(nb):
            P.op("dve", lambda e, b=b: e.scalar_tensor_tensor(hn1[s][0:bl, b, :], xt[s][0:bl, b, :], rs[0:bl, 4 + b:5 + b], wt1[0:bl, :],
                                                              ALU.mult, ALU.mult),
                 reads=[K("xt", s), K("rs1"), K("wt1")], writes=[K("hn1", 0)])
        if nt == 16:
            P.op("sp", lambda e: e.dma_start(out=a["h1"][r0:r0 + 16, :], in_=xt[s][0:16, 0, :]), reads=[K("xt", s)], dma_slot=pref + f"oh{s}")
            P.op("sp", lambda e: e.dma_start(out=a["hn1"][r0:r0 + 16, :], in_=hn1[s][0:16, 0, :]), reads=[K("hn1", 0)],
                 writes=[("d_hn1loc", "meta")], dma_slot=pref + "on")
        else:
            P.op("sp", lambda e: e.dma_start(out=a["h1"][r0:r0 + nt, :].rearrange("(b p) d -> p b d", p=128), in_=xt[s][:, :, :]),
                 reads=[K("xt", s)], dma_slot=pref + f"oh{s}")
            P.op("sp", lambda e: e.dma_start(out=a["hn1"][r0:r0 + nt, :].rearrange("(b p) d -> p b d", p=128), in_=hn1[s][:, :, :]),
                 reads=[K("hn1", 0)], writes=[("d_hn1loc", (r0 - 16) // 1024)], dma_slot=pref + "on")

    nT = len(tiles)
    load_x(0)
    norm_part(0)
    tr_part(0)
    for ti in range(nT):
        if ti + 1 < nT:
            load_x(ti + 1)
        inproj(ti, 0)
        for g in range(4):
            if g + 1 < 4:
                inproj(ti, g + 1)
            grp(ti, g)
            if g == 1 and ti + 1 < nT:
                norm_part(ti + 1)
        if ti + 1 < nT:
            tr_part(ti + 1)
        outproj(ti)
        if after_tile is not None:
            after_tile(ti)


def build_p2(nc, P, ps, a, ntiles, T=512, pref="p2", after_unit=None):
    sb = SB(nc, pref)
    K = lambda *k: (pref,) + k
    HG = 4
    W2K = [K("W2", k) for k in range(8)]
    W2 = sb.t("W2", [128, 8, 4096], BF16)
    ident = make_ident(P, sb, "ident")
    ones = sb.t("ones", [128, 128], BF16)
    smask = sb.t("smask", [128, T], F32)
    emask = sb.t("emask", [128, T], F32)
    emask16 = sb.t("emask16", [128, 16], F32)
    rdec = [sb.t(f"rdec{i}", [128, HG, 8], F32) for i in range(2)]
    trim = sb.t("trim", [128, 64], F32)
    trim2 = sb.t("trim2", [128, 64], F32)
    lg = sb.t("lg", [128, 2, 8], F32)
    lb = sb.t("lb", [128, 8], F32)
    homl = sb.t("homl", [128, 8], F32)
    nhoml = sb.t("nhoml", [128, 8], F32)
    hb = sb.t("hb", [128, 8], F32)
    onw = sb.t("onw", [128, 8], F32)
    hn1 = [sb.t("hn1_0", [128, 4, D], BF16)] * 2
    hnT = [sb.t(f"hnT{i}", [128, 8, T], BF16) for i in range(2)]
    sig = [sb.t(f"sig{i}", [128, T], F32) for i in range(2)]
    gb = [sb.t(f"gb{i}", [128, T], F32) for i in range(2)]
    bb = [sb.t(f"bb{i}", [128, T], F32) for i in range(2)]
    qs = [sb.t(f"qs{i}", [128, T], F32) for i in range(2)]
    khf = [sb.t(f"khf{i}", [128, T], BF16) for i in range(2)]
    qt = [sb.t(f"qt{i}", [128, HG, T], BF16) for i in range(2)]
    kt = [sb.t(f"kt{i}", [128, HG, T], BF16) for i in range(2)]
    kh = [sb.t(f"kh{i}", [128, 4, HG, 128], BF16) for i in range(2)]
    vt = [sb.t(f"vt{i}", [128, 4, HG, 128], BF16) for i in range(2)]
    sgt = [sb.t(f"sgt{i}", [128, HG, T], BF16) for i in range(2)]
    dec = [sb.t(f"dec{i}", [128, HG, 8], F32) for i in range(2)]
    S = sb.t("S", [128, 8, 128], F32)
    Sb = sb.t("Sb", [128, 8, 128], BF16)
    scm = [sb.t(f"scm{i}", [128, HG, 64], BF16) for i in range(2)]
    osb = sb.t("osb", [128, HG, T], F32)
    osq = [sb.t(f"osq{i}", [128, T], BF16) for i in range(2)]
    rsd = [sb.t(f"rsd{i}", [128, T], F32) for i in range(2)]
    mo = [sb.t(f"mo{i}", [128, HG, T], BF16) for i in range(2)]

    if a.get("w2b") is not None:
        for k in range(8):
            P.op("sp" if k % 2 == 0 else "act", lambda e, k=k: e.dma_start(out=W2[:, k, :], in_=a["w2b"][k * 128:(k + 1) * 128, :]),
                 reads=[("d_w2b", 0)], writes=[K("W2", k)], dma_slot=pref + f"w{k % 2}")
    else:
        P.op("pool", lambda e: [e.dma_start(out=W2[:, k, :], in_=a["w2"][k * 128:(k + 1) * 128, :]) for k in range(8)],
             writes=W2K, dma_slot=pref + "w0", npieces=8)
    P.op("sp", lambda e: e.dma_start(out=lg[:], in_=a["lbl"].rearrange("l (h p) -> p l h", p=128), allow_slow_non_contiguous=True),
         writes=[K("lg")], dma_slot=pref + "c0")
    P.op("sp", lambda e: e.dma_start(out=onw[:], in_=a["onorm"].rearrange("(h p) -> p h", p=128), allow_slow_non_contiguous=True),
         writes=[K("onw")], dma_slot=pref + "c1")
    P.op("pool", lambda e: e.memset(ones[:], 1.0), writes=[K("ones")])
    P.op("pool", lambda e: e.memset(smask[:], 0.0), writes=[K("smask")])
    P.op("pool", lambda e: e.memset(smask[:].rearrange("p (c t) -> p c t", t=64)[:, :, 0:1], 1.0), reads=[K("smask")], writes=[K("smask")])
    P.op("pool", lambda e: e.memset(emask[:], 0.0), writes=[K("emask")])
    P.op("pool", lambda e: e.memset(emask[:].rearrange("p (c t) -> p c t", t=64)[:, :, 63:64], 1.0), reads=[K("emask")], writes=[K("emask")])
    P.op("pool", lambda e: e.memset(emask16[:], 0.0), writes=[K("emask")])
    P.op("pool", lambda e: e.memset(emask16[:, 15:16], 1.0), reads=[K("emask")], writes=[K("emask")])
    P.op("pool", lambda e: e.memset(trim[:], 1.0), writes=[K("trim")])
    P.op("pool", lambda e: e.memset(trim2[:], 1.0), writes=[K("trim2")])
    P.op("pool", lambda e: e.affine_select(trim[:], trim[:], [[1, 64]], ALU.is_ge, 0.0, base=0, channel_multiplier=-1),
         reads=[K("trim")], writes=[K("trim")])
    P.op("pool", lambda e: e.affine_select(trim2[:], trim2[:], [[1, 64]], ALU.is_ge, 0.0, base=64, channel_multiplier=-1),
         reads=[K("trim2")], writes=[K("trim2")])
    P.op("pool", lambda e: e.tensor_copy(trim[64:128, :], trim2[64:128, :]), reads=[K("trim"), K("trim2")], writes=[K("trim")])
    P.op("pool", lambda e: e.memset(S[:], 0.0), writes=[K("S", h) for h in range(8)])
    P.op("pool", lambda e: e.memset(Sb[:], 0.0), writes=[K("Sb", h) for h in range(8)])
    P.op("dve", lambda e: e.tensor_tensor(lb[:], lg[:, 1, :], lg[:, 0, :], ALU.subtract), reads=[K("lg")], writes=[K("lb")])
    P.op("act", lambda e: e.activation(lb[:], lb[:], AF.Sigmoid), reads=[K("lb")], writes=[K("lb")])
    P.op("dve", lambda e: e.tensor_scalar(homl[:], lb[:], -0.5, 0.5, ALU.mult, ALU.add), reads=[K("lb")], writes=[K("homl")])
    P.op("dve", lambda e: e.tensor_scalar(nhoml[:], lb[:], 0.5, -0.5, ALU.mult, ALU.add), reads=[K("lb")], writes=[K("nhoml")])
    P.op("dve", lambda e: e.tensor_scalar(hb[:], lb[:], 0.5, 0.5, ALU.mult, ALU.add), reads=[K("lb")], writes=[K("hb")])
    P.op("dve", lambda e: e.tensor_scalar(onw[:], onw[:], float(np.sqrt(128.0)), None, ALU.mult), reads=[K("onw")], writes=[K("onw")])
    kc = [K("lb"), K("homl"), K("nhoml"), K("hb")]

    rot_proj = Rot([1, 2, 4])
    rot_sc = Rot([3])
    rot_o = Rot([5, 6])
    rot_scm = Rot([0, 1])
    rot_m = Rot([0, 1])

    tiles = [(0, 16)] + [(16 + T * i, T) for i in range(ntiles)]

    def load_dma(ti):
        r0, nt = tiles[ti]
        s = ti % 2
        if nt == 16:
            P.op("sp", lambda e: e.dma_start(out=hn1[s][0:16, 0, :], in_=a["hn_src"](ti)), reads=[a["hn_key"](ti)], writes=[K("hn1", 0)], dma_slot=pref + f"x{s}")
        else:
            P.op("sp", lambda e: e.dma_start(out=hn1[s][:, :, :], in_=a["hn_src"](ti).rearrange("(b p) d -> p b d", p=128)),
                 reads=[a["hn_key"](ti)], writes=[K("hn1", 0)], dma_slot=pref + f"x{s}")

    def load_T_block(ti, b):
        r0, nt = tiles[ti]
        s = ti % 2
        bl = min(nt, 128)
        pst = ps[0].bitcast(BF16)

        def tfn(e):
            ins = None
            for k in range(8):
                ins = e.transpose(pst[:, k * 128:k * 128 + bl], hn1[s][0:bl, b, k * 128:(k + 1) * 128], ident[0:bl, 0:bl])
            return ins
        P.op("pe", tfn, reads=[K("hn1", 0), pref + "ident"], writes=[("ps", 0)])
        P.op("act", lambda e: e.activation(hnT[s][:, :, b * 128:b * 128 + bl],
                                           pst[:, :].rearrange("p (k t) -> p k t", t=128)[:, :, 0:bl], AF.Copy),
             reads=[("ps", 0)], writes=[K("hnT", s)])

    def nblocks(ti):
        return max(1, tiles[ti][1] // 128)

    def stageA(ti, hg, up):
        r0, nt = tiles[ti]
        s = ti % 2
        nb = max(1, nt // 128)
        bl = min(nt, 128)
        nch = max(1, nt // 64)
        cl = min(nt, 64)
        c3 = lambda ap: ap.rearrange("p (c t) -> p c t", t=cl)
        rev = lambda t_: bass.AP(t_, nt - 1, [[T, 128], [-1, nt]])
        rev_em = bass.AP(emask16, 15, [[16, 128], [-1, 16]]) if nt == 16 else rev(emask)
        nxt = list(range(nblocks(ti + 1))) if (hg == 1 and ti + 1 < len(tiles)) else []

        def next_block():
            if nxt:
                b = nxt.pop(0)
                load_T_block(ti + 1, b)
                if not nxt and ti + 2 < len(tiles):
                    load_dma(ti + 2)
        for b in range(nb):
            bank = rot_proj.next()
            c0 = 2048 + hg * 512
            mm_group(P, ps[bank][0:bl, :], [(hnT[s][:, k, b * 128:b * 128 + bl], W2[:, k, c0:c0 + 512]) for k in range(8)],
                     reads=W2K + [K("hnT", s)], writes=[("ps", bank)])
            P.op("act", lambda e, b=b, bank=bank: e.activation(vt[up][0:bl, b, :, :], ps[bank][0:bl, :].rearrange("p (h v) -> p h v", v=128), AF.Copy),
                 reads=[("ps", bank)], writes=[K("vt", up)])
            if b % 2 == 1:
                next_block()
                next_block()
            yield
        for pair in range(2):
            hp = [(2 * pair + x, hg * HG + 2 * pair + x, x) for x in range(2)]
            for hh, h, st in hp:
                bank = rot_proj.next()
                mm_group(P, ps[bank][:, 0:nt], [(W2[:, k, 1024 + h * 128:1024 + (h + 1) * 128], hnT[s][:, k, 0:nt]) for k in range(8)],
                         reads=W2K + [K("hnT", s)], writes=[("ps", bank)])
                P.op("act", lambda e, bank=bank, st=st: e.activation(sig[st][:, 0:nt], ps[bank][:, 0:nt], AF.Tanh, scale=0.5),
                     reads=[("ps", bank)], writes=[K("sig", st)])
            for hh, h, st in hp:
                P.op("act", lambda e, st=st, h=h: e.activation(gb[st][:, 0:nt], sig[st][:, 0:nt], AF.Identity, bias=hb[:, h:h + 1], scale=homl[:, h:h + 1]),
                     reads=[K("sig", st)] + kc, writes=[K("gb", st)])
            for hh, h, st in hp:
                P.op("act", lambda e, st=st, h=h: e.activation(sig[st][:, 0:nt], sig[st][:, 0:nt], AF.Identity, bias=homl[:, h:h + 1], scale=nhoml[:, h:h + 1]),
                     reads=[K("sig", st)] + kc, writes=[K("sig", st)])
            next_block()
            yield
            for hh, h, st in hp:
                bank = rot_proj.next()
                mm_group(P, ps[bank][:, 0:nt], [(W2[:, k, h * 128:(h + 1) * 128], hnT[s][:, k, 0:nt]) for k in range(8)],
                         reads=W2K + [K("hnT", s)], writes=[("ps", bank)])
                P.op("act", lambda e, bank=bank, st=st: e.activation(qs[st][:, 0:nt], ps[bank][:, 0:nt], AF.Silu),
                     reads=[("ps", bank)], writes=[K("qs", st)])
            for hh, h, st in hp:
                P.op("dve", lambda e, st=st: e.tensor_tensor_scan(bb[st][:, 0:nt], smask[:, 0:nt], gb[st][:, 0:nt], 1.0, ALU.max, ALU.mult),
                     reads=[K("gb", st), K("smask")], writes=[K("bb", st)])
            for hh, h, st in hp:
                P.op("dve", lambda e, st=st: e.tensor_tensor_scan(rev(gb[st]), rev_em, rev(gb[st]), 1.0, ALU.max, ALU.mult),
                     reads=[K("gb", st), K("emask")], writes=[K("gb", st)])
            next_block()
            yield
            if nt != 16:
                for hh, h, st in hp:
                    bank = rot_proj.next()
                    mm_group(P, ps[bank][:, 0:nt], [(W2[:, k, 3072 + h * 128:3072 + (h + 1) * 128], hnT[s][:, k, 0:nt]) for k in range(8)],
                             reads=W2K + [K("hnT", s)], writes=[("ps", bank)])
                    P.op("act", lambda e, bank=bank, hh=hh: e.activation(sgt[up][:, hh, 0:nt], ps[bank][:, 0:nt], AF.Silu),
                         reads=[("ps", bank)], writes=[K("sgt", up)])
            for hh, h, st in hp:
                P.op("pool", lambda e, st=st, hh=hh: e.tensor_copy(dec[up][:, hh, 0:nch], c3(bb[st][:, 0:nt])[:, :, cl - 1]),
                     reads=[K("bb", st)], writes=[K("dec", up)])
            for hh, h, st in hp:
                P.op("pool", lambda e, st=st: e.tensor_tensor(c3(sig[st][:, 0:nt])[:, :, 0:cl - 1], c3(sig[st][:, 0:nt])[:, :, 0:cl - 1],
                                                              c3(gb[st][:, 0:nt])[:, :, 1:cl], ALU.mult),
                     reads=[K("sig", st), K("gb", st)], writes=[K("sig", st)])
            for hh, h, st in hp:
                P.op("dve", lambda e, hh=hh: e.tensor_scalar(rdec[up][:, hh, 0:nch], dec[up][:, hh, 0:nch], 1e-30, None, ALU.max),
                     reads=[K("dec", up)], writes=[K("rdec", up)])
                P.op("dve", lambda e, hh=hh: e.reciprocal(rdec[up][:, hh, 0:nch], rdec[up][:, hh, 0:nch]),
                     reads=[K("rdec", up)], writes=[K("rdec", up)])
            for hh, h, st in hp:
                P.op("pool", lambda e, st=st, hh=hh: e.tensor_tensor(qt[up][:, hh, 0:nt], qs[st][:, 0:nt], bb[st][:, 0:nt], ALU.mult),
                     reads=[K("qs", st), K("bb", st)], writes=[K("qt", up)])
            for hh, h, st in hp:
                P.op("act", lambda e, st=st: e.activation(khf[st][:, 0:nt], sig[st][:, 0:nt], AF.Copy), reads=[K("sig", st)], writes=[K("khf", st)])
            next_block()
            yield
            for hh, h, st in hp:
                P.op("pool", lambda e, st=st, hh=hh: e.tensor_tensor(c3(kt[up][:, hh, 0:nt]), c3(sig[st][:, 0:nt]),
                                                                     rdec[up][:, hh, 0:nch].unsqueeze(2).to_broadcast([128, nch, cl]), ALU.mult),
                     reads=[K("sig", st), K("rdec", up)], writes=[K("kt", up)])
            pst = ps[0].bitcast(BF16)

            def tfn(e, hp=hp):
                ins = None
                for x, (hh, h, st) in enumerate(hp):
                    for b in range(nb):
                        ins = e.transpose(pst[0:bl, (x * 4 + b) * 128:(x * 4 + b + 1) * 128], khf[st][:, b * 128:b * 128 + bl], ident[:, :])
                return ins
            P.op("pe", tfn, reads=[K("khf", 0), K("khf", 1), pref + "ident"], writes=[("ps", 0)])
            P.op("act", lambda e, pair=pair: e.activation(
                kh[up][0:bl, 0:nb, 2 * pair:2 * pair + 2, :].rearrange("p b x k -> p x b k"),
                pst[0:bl, :].rearrange("p (x b k) -> p x b k", x=2, k=128)[:, :, 0:nb, :], AF.Copy),
                reads=[("ps", 0)], writes=[K("kh", up)])
            next_block()
            yield
        while nxt:
            next_block()
            yield

    def stageBC(ti, hg, up):
        r0, nt = tiles[ti]
        nch = max(1, nt // 64)
        cl = min(nt, 64)
        hs = [hg * HG + hh for hh in range(HG)]
        for c in range(nch):
            blk, half = c // 2, c % 2
            p0 = half * 64
            cs = slice(c * 64, c * 64 + cl)
            if nt == 16:
                yield
            if nt != 16:
                bsc = rot_sc.next()

                def scfn(e, bsc=bsc, p0=p0, cs=cs):
                    ins = None
                    for hh in range(HG):
                        ins = e.matmul(ps[bsc][p0:p0 + cl, hh * 64:hh * 64 + cl], kt[up][:, hh, cs], qt[up][:, hh, cs], start=True, stop=True)
                    return ins
                P.op("pe", scfn, reads=[K("kt", up), K("qt", up)], writes=[("ps", bsc)])
                ms = rot_scm.next()
                P.op("dve", lambda e, bsc=bsc, ms=ms, p0=p0: e.tensor_tensor(
                    scm[ms][p0:p0 + cl, :, 0:cl],
                    ps[bsc][p0:p0 + cl, 0:HG * 64].rearrange("p (h t) -> p h t", t=64)[:, :, 0:cl],
                    trim[p0:p0 + cl, 0:cl].unsqueeze(1).to_broadcast([cl, HG, cl]), ALU.mult),
                    reads=[("ps", bsc), K("trim")], writes=[K("scm", ms)])
                yield
                bo = rot_o.next()

                def ofn(e, bo=bo, ms=ms, p0=p0, cs=cs, blk=blk):
                    ins = None
                    for hh in range(HG):
                        o_ap = ps[bo][:, hh * 64:hh * 64 + cl]
                        e.matmul(o_ap, vt[up][p0:p0 + cl, blk, hh, :], scm[ms][p0:p0 + cl, hh, 0:cl], start=True, stop=False)
                        ins = e.matmul(o_ap, Sb[:, hs[hh], :], qt[up][:, hh, cs], start=False, stop=True)
                    return ins
                P.op("pe", ofn, reads=[K("vt", up), K("scm", ms), K("qt", up)] + [K("Sb", h) for h in hs], writes=[("ps", bo)])
                P.op("act", lambda e, bo=bo, cs=cs: e.activation(osb[:, :, cs], ps[bo][:, 0:HG * 64].rearrange("p (h t) -> p h t", t=64)[:, :, 0:cl], AF.Copy),
                     reads=[("ps", bo)], writes=[K("osb")])

            def sfn(e, p0=p0, blk=blk):
                ins = None
                for hh in range(HG):
                    ins = e.matmul(ps[7][:, hh * 128:(hh + 1) * 128], kh[up][p0:p0 + cl, blk, hh, :], vt[up][p0:p0 + cl, blk, hh, :], start=True, stop=True)
                return ins
            P.op("pe", sfn, reads=[K("kh", up), K("vt", up)], writes=[("ps", 7)])
            for hh in range(HG):
                h = hs[hh]
                P.op("dve", lambda e, h=h, hh=hh, c=c: e.scalar_tensor_tensor(Sb[:, h, :], S[:, h, :], dec[up][:, hh, c:c + 1],
                                                                             ps[7][:, hh * 128:(hh + 1) * 128], ALU.mult, ALU.add),
                     reads=[("ps", 7), K("S", h), K("dec", up)], writes=[K("Sb", h)])
            for hh in range(HG):
                h = hs[hh]
                P.op("dve", lambda e, h=h, hh=hh, c=c: e.scalar_tensor_tensor(S[:, h, :], S[:, h, :], dec[up][:, hh, c:c + 1],
                                                                             ps[7][:, hh * 128:(hh + 1) * 128], ALU.mult, ALU.add),
                     reads=[("ps", 7), K("S", h), K("dec", up)], writes=[K("S", h)])
            yield
        if nt == 16:
            return
        mp = rot_m.next()
        hps = [[(2 * pair + x, hs[2 * pair + x], x) for x in range(2)] for pair in range(2)]
        banks = {}

        def c_square(hp):
            for hh, h, st in hp:
                P.op("act", lambda e, hh=hh, st=st: e.activation(osq[st][:, :], osb[:, hh, :], AF.Square), reads=[K("osb")], writes=[K("osq", st)])

        def c_norm(hp):
            for hh, h, st in hp:
                bank = rot_proj.next()
                banks[st] = bank
                P.op("pe", lambda e, st=st, bank=bank: e.matmul(ps[bank][:, :], ones[:, :], osq[st][:, :], start=True, stop=True),
                     reads=[K("ones"), K("osq", st)], writes=[("ps", bank)])
            for hh, h, st in hp:
                P.op("act", lambda e, st=st, bank=banks[st]: e.activation(rsd[st][:, :], ps[bank][:, :], AF.Ln, bias=128.0 * EPS),
                     reads=[("ps", banks[st])], writes=[K("rsd", st)])
            for hh, h, st in hp:
                P.op("act", lambda e, st=st: e.activation(rsd[st][:, :], rsd[st][:, :], AF.Exp, scale=-0.5),
                     reads=[K("rsd", st)], writes=[K("rsd", st)])

        def c_out(hp):
            for hh, h, st in hp:
                P.op("dve", lambda e, hh=hh, h=h, st=st: e.scalar_tensor_tensor(rsd[st][:, :], osb[:, hh, :], onw[:, h:h + 1], rsd[st][:, :], ALU.mult, ALU.mult),
                     reads=[K("osb"), K("onw"), K("rsd", st)], writes=[K("rsd", st)])
            for hh, h, st in hp:
                P.op("dve", lambda e, hh=hh, st=st: e.tensor_tensor(mo[mp][:, hh, :], rsd[st][:, :], sgt[up][:, hh, :], ALU.mult),
                     reads=[K("rsd", st), K("sgt", up)], writes=[K("mo", mp)])
        c_square(hps[0])
        yield
        c_norm(hps[0])
        yield
        c_out(hps[0])
        c_square(hps[1])
        yield
        c_norm(hps[1])
        yield
        c_out(hps[1])
        P.op("sp", lambda e: e.dma_start(out=a["m_dst"](ti, hg), in_=mo[mp][:, :, :]),
             reads=[K("mo", mp)], writes=[a["m_key"](ti)], dma_slot=pref + f"om{mp}")
        if after_unit is not None:
            after_unit(ti, hg)
        yield

    units = [(ti, hg) for ti in range(len(tiles)) for hg in range(2)]
    load_dma(0)
    for b in range(nblocks(0)):
        load_T_block(0, b)
    if len(tiles) > 1:
        load_dma(1)
    prevBC = None
    for n, (ti, hg) in enumerate(units):
        gA = stageA(ti, hg, n % 2)
        gB = prevBC
        aliveA, aliveB = True, gB is not None

        def step(g):
            try:
                next(g)
                return True
            except StopIteration:
                return False
        if aliveB:
            aliveA = step(gA)
        while aliveA or aliveB:
            if aliveB:
                aliveB = step(gB)
            if aliveA:
                aliveA = step(gA)
            if aliveB:
                aliveB = step(gB)
        prevBC = stageBC(ti, hg, n % 2)
    for _ in prevBC:
        pass


def build_p3(nc, P, ps, a, ntiles, T=512, pref="p3"):
    sb = SB(nc, pref)
    K = lambda *k: (pref,) + k
    NB = T // 128
    WOK = [K("W_out", k) for k in range(16)]
    W_out = sb.t("W_out", [128, 16, 1024], BF16)
    wtf = sb.t("wtf", [128, D], F32)
    mt = [sb.t(f"mt{i}", [128, 16, T], BF16) for i in range(2)]
    ht = [sb.t(f"ht{i}", [128, NB, D], F32) for i in range(2)]
    ot = [sb.t(f"ot{i}", [128, NB, D], F32) for i in range(2)]
    junk = sb.t("junk", [128, D], BF16)
    ss = sb.t("ss", [128, 8], F32)
    rs = sb.t("rs", [128, 8], F32)
    if a.get("wob") is not None:
        for k in range(16):
            P.op("sp" if k % 2 == 0 else "act", lambda e, k=k: e.dma_start(out=W_out[:, k, :], in_=a["wob"][k * 128:(k + 1) * 128, :]),
                 reads=[("d_wob", 0)], writes=[K("W_out", k)], dma_slot=pref + f"w{k % 2}")
    else:
        P.op("pool", lambda e: [e.dma_start(out=W_out[:, k, :], in_=a["w_out"][k * 128:(k + 1) * 128, :]) for k in range(16)],
             writes=WOK, dma_slot=pref + "w0", npieces=16)
    P.op("sp", lambda e: e.dma_start(out=wtf[:], in_=a["nwf"].partition_broadcast(128)), writes=[K("wtf")], dma_slot=pref + "c0")
    P.op("dve", lambda e: e.tensor_scalar(wtf[:], wtf[:], float(np.sqrt(float(D))), None, ALU.mult), reads=[K("wtf")], writes=[K("wtf")])
    rot_out = Rot([0, 1, 2, 3])

    def load(i):
        s = i % 2
        P.op("sp", lambda e: [e.dma_start(out=mt[s][:, k0:k1, :], in_=src) for (k0, k1, src) in a["m_src"](e, i)],
             reads=list(a["m_keys"](i)), writes=[K("mt", s)], dma_slot=pref + f"m{s}", npieces=a["m_npieces"])
        P.op("sp", lambda e: e.dma_start(out=ht[s][:, :, :], in_=a["h1"][16 + i * T:16 + (i + 1) * T, :].rearrange("(b p) d -> p b d", p=128)),
             writes=[K("ht", s)], dma_slot=pref + f"h{s}")

    def compute(i):
        s = i % 2
        for b in range(NB):
            for hf in range(2):
                bank = rot_out.next()
                mm_group(P, ps[bank][:, :], [(mt[s][:, k, b * 128:(b + 1) * 128], W_out[:, k, hf * 512:(hf + 1) * 512]) for k in range(16)],
                         reads=[K("mt", s)] + WOK, writes=[("ps", bank)])
                P.op("dve", lambda e, b=b, hf=hf, bank=bank: e.tensor_tensor(ht[s][:, b, hf * 512:(hf + 1) * 512],
                                                                           ht[s][:, b, hf * 512:(hf + 1) * 512], ps[bank][:, :], ALU.add),
                     reads=[("ps", bank), K("ht", s)], writes=[K("ht", s)])
            P.op("act", lambda e, b=b: e.activation(junk[:, :], ht[s][:, b, :], AF.Square, accum_out=ss[:, b:b + 1]),
                 reads=[K("ht", s)], writes=[K("junk"), K("ss")])
        rms_rstd(P, ss[:, 0:NB], rs[:, 0:NB], D, [K("ss")], [K("rs")])
        for b in range(NB):
            P.op("dve", lambda e, b=b: e.scalar_tensor_tensor(ot[s][:, b, :], ht[s][:, b, :], rs[:, b:b + 1], wtf[:, :], ALU.mult, ALU.mult),
                 reads=[K("ht", s), K("rs"), K("wtf")], writes=[K("ot", s)])
        P.op("sp", lambda e: e.dma_start(out=a["out"][i * T:(i + 1) * T, :].rearrange("(b p) d -> p b d", p=128), in_=ot[s][:, :, :]),
             reads=[K("ot", s)], dma_slot=pref + f"o{s}")

    load(0)
    for i in range(ntiles):
        if i + 1 < ntiles:
            load(i + 1)
        compute(i)


def _psum(nc, es):
    return [es.enter_context(nc.psum_tensor(f"psb{i}", [128, 512], F32)) for i in range(8)]


def make_p1(ntiles=16, T=256):
    nc = bass.Bass("TRN2", target_bir_lowering=False, dynamic_dma_scratch_size=SCRATCH)
    dt = lambda n, s, d, k: nc.dram_tensor(n, s, d, kind=k).ap()
    a = dict(
        xin=dt("xin", [LTOK, D], F32, "ExternalInput"),
        nw0=dt("nw0", [D], F32, "ExternalInput"), nw1=dt("nw1", [D], F32, "ExternalInput"),
        w_in=dt("w_in", [D, 4096], F32, "ExternalInput"), w_grp=dt("w_grp", [4, 512, 512], F32, "ExternalInput"),
        scale=dt("scale", [2048], F32, "ExternalInput"), w_out=dt("w_out", [2048, D], F32, "ExternalInput"),
        h1=dt("h1", [LTOK, D], F32, "ExternalOutput"), hn1=dt("hn1", [LTOK, D], BF16, "ExternalOutput"),
    )
    with ExitStack() as es:
        ps = _psum(nc, es)
        P = Prog(nc)
        build_p1(nc, P, ps, a, ntiles, T)
        P.finish(es)
    return nc


def make_p2(ntiles=16, T=512):
    nc = bass.Bass("TRN2", target_bir_lowering=False, dynamic_dma_scratch_size=SCRATCH)
    dt = lambda n, s, d, k: nc.dram_tensor(n, s, d, kind=k).ap()
    a = dict(
        hn1=dt("hn1", [LFULL, D], BF16, "ExternalInput"),
        w2=dt("w2", [D, 4096], F32, "ExternalInput"),
        lbl=dt("lbl", [2, 1024], F32, "ExternalInput"),
        onorm=dt("onorm", [1024], F32, "ExternalInput"),
        m=dt("m", [8, 128, SEQ], BF16, "ExternalOutput"),
    )
    a["hn_src"] = lambda ti: a["hn1"][0:16, :] if ti == 0 else a["hn1"][16 + T * (ti - 1):16 + T * ti, :]
    a["hn_key"] = lambda ti: ("d_hn1all", 0)
    a["m_dst"] = lambda ti, hg: a["m"][hg * 4:(hg + 1) * 4, :, T * (ti - 1):T * ti].rearrange("h p t -> p h t")
    a["m_key"] = lambda ti: ("d_mloc", 0)
    with ExitStack() as es:
        ps = _psum(nc, es)
        P = Prog(nc)
        build_p2(nc, P, ps, a, ntiles, T)
        P.finish(es)
    return nc


def make_p3(ntiles=8, T=512):
    nc = bass.Bass("TRN2", target_bir_lowering=False, dynamic_dma_scratch_size=SCRATCH)
    dt = lambda n, s, d, k: nc.dram_tensor(n, s, d, kind=k).ap()
    a = dict(
        m=dt("m", [16, 128, HALF], BF16, "ExternalInput"),
        h1=dt("h1", [LTOK, D], F32, "ExternalInput"),
        w_out=dt("w_out", [2048, D], F32, "ExternalInput"),
        nwf=dt("nwf", [D], F32, "ExternalInput"),
        out=dt("out", [HALF, D], F32, "ExternalOutput"),
    )
    a["m_src"] = lambda e, i: [(0, 16, a["m"][:, :, i * T:(i + 1) * T].rearrange("h p t -> p h t"))]
    a["m_npieces"] = 1
    a["m_keys"] = lambda i: [("d_mall", 0)]
    with ExitStack() as es:
        ps = _psum(nc, es)
        P = Prog(nc)
        build_p3(nc, P, ps, a, ntiles, T)
        P.finish(es)
    return nc


GROUPS = [[0, 1], [2, 3], [4, 5], [6, 7]]


def make_fused(nt1=16, nt2=16, nt3=8, T1=256, T2=512, T3=512):
    nc = bass.Bass("TRN2", target_bir_lowering=False, dynamic_dma_scratch_size=SCRATCH)
    dt = lambda n, s, d, k="Internal": nc.dram_tensor(n, s, d, kind=k).ap()
    EI, EO = "ExternalInput", "ExternalOutput"
    rows = [1024, 1024, 1024, 1024]
    row0 = [NMETA, NMETA + 1024, NMETA + 2048, NMETA + 3072]
    hn1meta = dt("hn1meta", [2 * NMETA, D], BF16)
    h1 = dt("h1", [LTOK, D], F32)
    hn1loc = dt("hn1loc", [LTOK, D], BF16)
    hn1all = [dt(f"hn1all{j}", [2 * rows[j], D], BF16) for j in range(4)]
    mloc = dt("mloc", [2, 4, 128, 2, 8, T2], BF16)
    mall = dt("mall", [2, 4, 2, 128, 2, 8, T2], BF16)
    a1 = dict(
        xin=dt("xin", [LTOK, D], F32, EI), nw0=dt("nw0", [D], F32, EI), nw1=dt("nw1", [D], F32, EI),
        w_in=dt("w_in", [D, 4096], F32, EI), w_grp=dt("w_grp", [4, 512, 512], F32, EI),
        scale=dt("scale", [2048], F32, EI), w_out=dt("w_out0", [2048, D], F32, EI),
        h1=h1, hn1=hn1loc)
    a2 = dict(w2=dt("w2", [D, 4096], F32, EI), lbl=dt("lbl", [2, 1024], F32, EI), onorm=dt("onorm", [1024], F32, EI))
    a3 = dict(h1=h1, w_out=dt("w_out1", [2048, D], F32, EI), nwf=dt("nwf", [D], F32, EI), out=dt("out", [HALF, D], F32, EO))
    a2["w2b"] = dt("w2b", [D, 4096], BF16)
    a3["wob"] = dt("wob", [2048, D], BF16)

    def hn_loc(ti):
        if ti == 0:
            return "meta", 0
        i = ti - 1
        rank, li = i // 8, T2 * (i % 8)
        j = li // 1024
        return j, rank * rows[j] + li - 1024 * j
    def hn_src(ti):
        j, r = hn_loc(ti)
        if ti == 0:
            return hn1meta[0:NMETA, :]
        return hn1all[j][r:r + T2, :]
    a2["hn_src"] = hn_src
    a2["hn_key"] = lambda ti: ("d_hn1all", hn_loc(ti)[0])
    def m_piece(ti):
        i = ti - 1
        return i // 8, (i % 8) // 2, i % 2
    def m_dst(ti, hg):
        half, q, w = m_piece(ti)
        return mloc[half, q, :, w, hg * 4:(hg + 1) * 4, :]
    a2["m_dst"] = m_dst
    a2["m_key"] = lambda ti: ("d_mloc", m_piece(ti)[0] * 4 + m_piece(ti)[1])
    par_cache = {}

    def m_src(e, i):
        if "p" not in par_cache:
            par_cache["p"] = e.snap(e.partition_id() % 2)
        par = par_cache["p"]
        q, w = i // 2, i % 2
        return [(8 * r, 8 * r + 8, mall[bass.ds(par, 1), q, r, :, w, :, :].rearrange("a p h t -> p (a h) t")) for r in range(2)]
    a3["m_src"] = m_src
    a3["m_npieces"] = 2
    a3["m_keys"] = lambda i: [("d_mall", 0, i // 2), ("d_mall", 1, i // 2)]

    with ExitStack() as es:
        ps = _psum(nc, es)
        P = Prog(nc)
        pend = []

        def flush():
            while pend:
                pend.pop(0)()

        def gather_h(j):
            P.op("pool", lambda e: e.collective_compute("AllGather", ALU.bypass, replica_groups=GROUPS,
                                                        ins=[hn1loc[row0[j]:row0[j] + rows[j], :]], outs=[hn1all[j]]),
                 reads=[("d_hn1loc", j)], writes=[("d_hn1all", j)], dma_slot=f"cch{j}", inc=1, exempt=True)

        def gather_meta():
            P.op("pool", lambda e: e.collective_compute("AllGather", ALU.bypass, replica_groups=GROUPS,
                                                        ins=[hn1loc[0:NMETA, :]], outs=[hn1meta]),
                 reads=[("d_hn1loc", "meta")], writes=[("d_hn1all", "meta")], dma_slot="cchm", inc=1, exempt=True)

        def after_tile(ti):
            flush()
            if 3 <= ti < 15:
                precast_piece(ti - 3)
            if ti == 0:
                pend.append(gather_meta)
            if ti >= 1 and ti % 4 == 0:
                j = ti // 4 - 1
                pend.append(lambda j=j: gather_h(j))
        def precast_piece(n):
            if n < 8:
                P.op("pool", lambda e: e.dma_start(out=a2["w2b"][n * 128:(n + 1) * 128, :], in_=a2["w2"][n * 128:(n + 1) * 128, :]),
                     writes=[("d_w2b", 0)], dma_slot="pc2", exempt=True)
            elif n < 12:
                k = n - 8
                P.op("pool", lambda e: e.dma_start(out=a3["wob"][k * 512:(k + 1) * 512, :], in_=a3["w_out"][k * 512:(k + 1) * 512, :]),
                     writes=[("d_wob", 0)], dma_slot="pc3", exempt=True)
        import os
        PH = os.environ.get("FUSE_PHASES", "123")
        NOG = os.environ.get("FUSE_NOGATHER", "0") == "1"
        if NOG:
            gather_h = lambda j: None
            gather_meta = lambda: None
        if "1" in PH:
            build_p1(nc, P, ps, a1, nt1, T1, after_tile=after_tile)
        flush()
        P.barrier()

        def gather_m(half, q):
            P.op("pool", lambda e: e.collective_compute("AllGather", ALU.bypass, replica_groups=GROUPS,
                                                        ins=[mloc[half, q].rearrange("p w h t -> p (w h t)")],
                                                        outs=[mall[half, q].rearrange("r p w h t -> (r p) (w h t)")]),
                 reads=[("d_mloc", half * 4 + q)], writes=[("d_mall", half, q)], dma_slot=f"ccm{half}{q}", inc=1, exempt=True)

        def after_unit(ti, hg):
            flush()
            i = ti - 1
            if hg == 1 and i % 2 == 1:
                half, q, _ = m_piece(ti)
                pend.append(lambda half=half, q=q: gather_m(half, q))
        if NOG:
            gather_m = lambda half, q: None
        if "2" in PH:
            build_p2(nc, P, ps, a2, nt2, T2, after_unit=after_unit)
        flush()
        P.barrier()
        if "3" in PH:
            build_p3(nc, P, ps, a3, nt3, T3)
        P.finish(es)
    return nc


def fused_inputs(x, meta_tokens, norm_w, pool_w_in, pool_w_grp, pool_scale, pool_w_out,
                 hgrn_w_in, hgrn_lb_logits, hgrn_o_norm, hgrn_w_out, final_norm_w):
    m1 = p1_inputs(x, meta_tokens, norm_w, pool_w_in, pool_w_grp, pool_scale, pool_w_out)
    maps = []
    for c in range(8):
        d = dict(m1[c])
        d["w_out0"] = d.pop("w_out")
        d.update(p2_weights(c, hgrn_w_in, hgrn_lb_logits, hgrn_o_norm))
        d["w_out1"] = hgrn_w_out[0]
        d["nwf"] = final_norm_w
        maps.append(d)
    return maps


def p1_inputs(x, meta_tokens, norm_w, pool_w_in, pool_w_grp, pool_scale, pool_w_out):
    maps = []
    for c in range(8):
        b, half = c // 2, c % 2
        if half == 0:
            xin = np.concatenate([meta_tokens, x[b, 0:HALF]], axis=0)
        else:
            xin = x[b, HALF - NMETA:SEQ]
        maps.append(dict(xin=np.ascontiguousarray(xin), nw0=norm_w[0], nw1=norm_w[1], w_in=pool_w_in[0], w_grp=pool_w_grp[0],
                         scale=pool_scale[0], w_out=pool_w_out[0]))
    return maps


def p2_weights(c, hgrn_w_in, hgrn_lb_logits, hgrn_o_norm):
    hg = c % 2
    w = hgrn_w_in[0]
    cols = [w[:, part * 2048 + hg * 1024: part * 2048 + (hg + 1) * 1024] for part in range(4)]
    return dict(w2=np.ascontiguousarray(np.concatenate(cols, axis=1)),
                lbl=np.ascontiguousarray(hgrn_lb_logits[:, hg * 1024:(hg + 1) * 1024]),
                onorm=np.ascontiguousarray(hgrn_o_norm[0, hg * 1024:(hg + 1) * 1024]))


def kernel_unfused(x, meta_tokens, norm_w, pool_w_in, pool_w_grp, pool_scale, pool_w_out,
                   hgrn_w_in, hgrn_lb_logits, hgrn_o_norm, hgrn_w_out, final_norm_w):
    f = lambda t: np.asarray(t, dtype=np.float32)
    x, meta_tokens, norm_w = f(x), f(meta_tokens), f(norm_w)
    cores = list(range(8))
    nc1 = make_p1()
    r1 = run_bass_kernel_spmd(nc1, p1_inputs(x, meta_tokens, norm_w, f(pool_w_in), f(pool_w_grp), f(pool_scale), f(pool_w_out)),
                              core_ids=cores).results
    nc2 = make_p2()
    maps2 = []
    for c in cores:
        b = c // 2
        hn_full = np.concatenate([r1[2 * b]["hn1"], r1[2 * b + 1]["hn1"][NMETA:]], axis=0)
        d = p2_weights(c, f(hgrn_w_in), f(hgrn_lb_logits), f(hgrn_o_norm))
        d["hn1"] = np.ascontiguousarray(hn_full)
        maps2.append(d)
    r2 = run_bass_kernel_spmd(nc2, maps2, core_ids=cores).results
    nc3 = make_p3()
    maps3 = []
    for c in cores:
        b, half = c // 2, c % 2
        m = np.concatenate([r2[2 * b]["m"][:, :, half * HALF:(half + 1) * HALF],
                            r2[2 * b + 1]["m"][:, :, half * HALF:(half + 1) * HALF]], axis=0)
        maps3.append(dict(m=np.ascontiguousarray(m), h1=r1[c]["h1"], w_out=f(hgrn_w_out)[0], nwf=f(final_norm_w)))
    r3 = run_bass_kernel_spmd(nc3, maps3, core_ids=cores).results
    out = np.empty((4, SEQ, D), np.float32)
    for c in cores:
        b, half = c // 2, c % 2
        out[b, half * HALF:(half + 1) * HALF] = r3[c]["out"]
    return out


def kernel(x, meta_tokens, norm_w, pool_w_in, pool_w_grp, pool_scale, pool_w_out,
           hgrn_w_in, hgrn_lb_logits, hgrn_o_norm, hgrn_w_out, final_norm_w):
    f = lambda t: np.ascontiguousarray(np.asarray(t, dtype=np.float32))
    maps = fused_inputs(f(x), f(meta_tokens), f(norm_w), f(pool_w_in), f(pool_w_grp), f(pool_scale), f(pool_w_out),
                        f(hgrn_w_in), f(hgrn_lb_logits), f(hgrn_o_norm), f(hgrn_w_out), f(final_norm_w))
    nc = make_fused()
    res = run_bass_kernel_spmd(nc, maps, core_ids=list(range(8))).results
    out = np.empty((4, SEQ, D), np.float32)
    for c in range(8):
        b, half = c // 2, c % 2
        out[b, half * HALF:(half + 1) * HALF] = res[c]["out"]
    return out
```

```python
from contextlib import ExitStack
import numpy as np
import ml_dtypes
import concourse.bass as bass
import concourse.mybir as mybir
from concourse.bass_utils import run_bass_kernel_spmd

F32 = mybir.dt.float32
BF16 = mybir.dt.bfloat16
AF = mybir.ActivationFunctionType
ALU = mybir.AluOpType

D = 1024
NMETA = 16
SEQ = 8192
HALF = SEQ // 2
EPS = 1e-6
LTOK = NMETA + HALF
LFULL = NMETA + SEQ

SCRATCH = 4096

ENGS = ("pe", "act", "dve", "pool", "sp")


class Prog:
    def __init__(self, nc):
        self.nc = nc
        self.ops = []
        self.phase = 0

    def barrier(self):
        self.phase += 1

    def op(self, eng, fn, reads=(), writes=(), dma_slot=None, npieces=1, inc=16, exempt=False):
        assert eng in ENGS
        self.ops.append(dict(eng=eng, fn=fn, reads=tuple(reads), writes=tuple(writes),
                             dma_slot=dma_slot, npieces=npieces, inc=inc, phase=self.phase, exempt=exempt))

    def finish(self, es):
        nc = self.nc
        ops = self.ops
        last_writer = {}
        readers = {}
        prev_last, cur_last, cur_phase = {}, {}, 0
        for i, o in enumerate(ops):
            if o["phase"] != cur_phase:
                prev_last.update(cur_last)
                cur_last, cur_phase = {}, o["phase"]
            deps = set(prev_last.values())
            if not o["exempt"]:
                cur_last[("dma", o["dma_slot"]) if o["dma_slot"] is not None else o["eng"]] = i
            for k in o["reads"]:
                if k in last_writer:
                    deps.add(last_writer[k])
            for k in o["writes"]:
                if k in last_writer:
                    deps.add(last_writer[k])
                for r in readers.get(k, ()):
                    deps.add(r)
            deps.discard(i)
            o["deps"] = deps
            for k in o["reads"]:
                readers.setdefault(k, []).append(i)
            for k in o["writes"]:
                last_writer[k] = i
                readers[k] = []

        def chan(o):
            return ("dma", o["dma_slot"]) if o["dma_slot"] is not None else o["eng"]

        for o in ops:
            o["marked"] = False
        for i, o in enumerate(ops):
            best = {}
            for d in o["deps"]:
                so = ops[d]
                c = chan(so)
                if so["dma_slot"] is None and so["eng"] == "pe" and o["eng"] == "pe" and o["dma_slot"] is None:
                    continue
                if c not in best or best[c] < d:
                    best[c] = d
            o["need"] = best
            for d in best.values():
                ops[d]["marked"] = True
        cnt = {}
        for o in ops:
            c = chan(o)
            if o["dma_slot"] is not None:
                cnt[c] = cnt.get(c, 0) + o["inc"] * o["npieces"]
                o["count"] = cnt[c]
            elif o["marked"]:
                cnt[c] = cnt.get(c, 0) + 1
                o["count"] = cnt[c]
        sems = {}
        for c in cnt:
            nm = "s_" + (c if isinstance(c, str) else "d_" + str(c[1]))
            sems[c] = es.enter_context(nc.semaphore(nm))
        engobj = {"pe": "tensor", "act": "scalar", "dve": "vector", "pool": "gpsimd", "sp": "sync"}
        with nc.Block() as block:
            for e in ENGS:
                my = [o for o in ops if o["eng"] == e]
                if not my:
                    continue

                def body(eng, my=my, e=e):
                    waited = {}
                    for o in my:
                        for c, d in sorted(o["need"].items(), key=lambda kv: str(kv[0])):
                            v = ops[d]["count"]
                            if waited.get(c, 0) < v:
                                eng.wait_ge(sems[c], v)
                                waited[c] = v
                        if o["dma_slot"] is not None:
                            insts = o["fn"](eng)
                            if not isinstance(insts, (list, tuple)):
                                insts = [insts]
                            assert len(insts) == o["npieces"], (len(insts), o["npieces"])
                            for ins in insts:
                                ins.then_inc(sems[("dma", o["dma_slot"])], o["inc"])
                        else:
                            ins = o["fn"](eng)
                            if o["marked"]:
                                ins.then_inc(sems[e], 1)
                    final = {}
                    for o in my:
                        if o["dma_slot"] is not None:
                            c = ("dma", o["dma_slot"])
                            final[c] = max(final.get(c, 0), o["count"])
                    for c, v in final.items():
                        if waited.get(c, 0) < v:
                            eng.wait_ge(sems[c], v)

                getattr(block, engobj[e])(body)


class SB:
    def __init__(self, nc, pref, base=SCRATCH, limit=192 * 1024):
        self.nc, self.pref, self.off, self.limit = nc, pref, base, limit

    def t(self, name, shape, dt):
        n = 1
        for s in shape[1:]:
            n *= s
        nbytes = n * mybir.dt.size(dt)
        nbytes = (nbytes + 63) // 64 * 64
        h = self.nc.alloc_sbuf_tensor_at(self.pref + name, list(shape), dt, offset=self.off)
        self.off += nbytes
        assert self.off <= self.limit, (self.pref, name, self.off)
        return h


class Rot:
    def __init__(self, items):
        self.items, self.i = list(items), 0

    def next(self):
        x = self.items[self.i % len(self.items)]
        self.i += 1
        return x


def mm_group(P, out_ap, pairs, reads, writes):
    def fn(e, out_ap=out_ap, pairs=pairs):
        n = len(pairs)
        ins = None
        for i, (l, r) in enumerate(pairs):
            ins = e.matmul(out_ap, l, r, start=(i == 0), stop=(i == n - 1))
        return ins
    P.op("pe", fn, reads=reads, writes=writes)


def load_weight_bf16(P, eng, dst, src, key, slot):
    kt = dst.shape[1]
    def fn(e):
        return [e.dma_start(out=dst[:, k, :], in_=src[k * 128:(k + 1) * 128, :]) for k in range(kt)]
    P.op(eng, fn, writes=[key], dma_slot=slot, npieces=kt)


def load_weight_cols(P, eng, dst, src, c0, c1, key, slot):
    kt = dst.shape[1]
    def fn(e):
        return [e.dma_start(out=dst[:, k, c0:c1], in_=src[k * 128:(k + 1) * 128, c0:c1]) for k in range(kt)]
    P.op(eng, fn, writes=[key], dma_slot=slot, npieces=kt)


def make_ident(P, sb, name):
    ident = sb.t(name, [128, 128], BF16)
    key = sb.pref + name
    P.op("pool", lambda e: e.memset(ident[:], 1.0), writes=[key])
    P.op("pool", lambda e: e.affine_select(ident[:], ident[:], [[-1, 128]], ALU.is_equal, 0.0,
                                           base=0, channel_multiplier=1), reads=[key], writes=[key])
    return ident


def rms_rstd(P, ss_ap, rs_ap, n, keys_r, keys_w):
    P.op("act", lambda e: e.activation(rs_ap, ss_ap, AF.Sqrt, bias=float(n) * EPS), reads=keys_r, writes=keys_w)
    P.op("dve", lambda e: e.reciprocal(rs_ap, rs_ap), reads=keys_w, writes=keys_w)


def build_p1(nc, P, ps, a, ntiles, T=256, pref="p1", after_tile=None):
    sb = SB(nc, pref)
    K = lambda *k: (pref,) + k
    NB = T // 128
    W_in = sb.t("W_in", [128, 8, 4096], BF16)
    W_grp = sb.t("W_grp", [128, 16, 512], BF16)
    W_out = sb.t("W_out", [128, 16, 1024], BF16)
    wt0 = sb.t("wt0", [128, D], F32)
    wt1 = sb.t("wt1", [128, D], F32)
    scale = sb.t("scale", [128, 16], F32)
    ident = make_ident(P, sb, "ident")
    xt = [sb.t(f"xt{i}", [128, NB, D], F32) for i in range(2)]
    hn = [sb.t(f"hn{i}", [128, D], BF16) for i in range(2)]
    hnT = [sb.t(f"hnT{i}", [128, 8, T], BF16) for i in range(2)]
    NV = 4
    vb = [sb.t(f"vb{i}", [128, 16 + T], F32) for i in range(NV)]
    sA = [sb.t(f"sA{i}", [128, 16 + T], F32) for i in range(NV)]
    sB = [sb.t(f"sB{i}", [128, 16 + T], F32) for i in range(NV)]
    halo = sb.t("halo", [128, 16, 16], F32)
    u = [sb.t(f"u{i}", [128, 4, T], BF16) for i in range(2)]
    sg = [sb.t(f"sg{i}", [128, 4, T], BF16) for i in range(2)]
    m0 = sb.t("m0", [128, 16, T], BF16)
    hn1 = [sb.t("hn1_0", [128, NB, D], BF16)] * 2
    junk = sb.t("junk", [128, D], BF16)
    ss = sb.t("ss", [128, 8], F32)
    rs = sb.t("rs", [128, 8], F32)
    rc = sb.t("rc", [128, 4, 16], F32)

    P.op("sp", lambda e: e.dma_start(out=wt0[:], in_=a["nw0"].partition_broadcast(128)), writes=[K("wt0")], dma_slot=pref + "c0")
    P.op("sp", lambda e: e.dma_start(out=wt1[:], in_=a["nw1"].partition_broadcast(128)), writes=[K("wt1")], dma_slot=pref + "c1")
    P.op("sp", lambda e: e.dma_start(out=scale[:], in_=a["scale"].rearrange("(j p) -> p j", p=128),
                                     allow_slow_non_contiguous=True), writes=[K("scale")], dma_slot=pref + "c2")
    sqD = float(np.sqrt(float(D)))
    P.op("dve", lambda e: e.tensor_scalar(wt0[:], wt0[:], sqD, None, ALU.mult), reads=[K("wt0")], writes=[K("wt0")])
    P.op("dve", lambda e: e.tensor_scalar(wt1[:], wt1[:], sqD, None, ALU.mult), reads=[K("wt1")], writes=[K("wt1")])
    P.op("pool", lambda e: e.memset(halo[:], 0.0), writes=[K("halo", j) for j in range(16)])
    load_weight_bf16(P, "pool", W_in, a["w_in"], K("W_in"), pref + "w0")
    P.op("pool", lambda e: [e.dma_start(out=W_grp[:, g * 4 + kk, :], in_=a["w_grp"][g, kk * 128:(kk + 1) * 128, :])
                            for g in range(4) for kk in range(4)], writes=[K("W_grp")], dma_slot=pref + "w1", npieces=16)
    load_weight_bf16(P, "pool", W_out, a["w_out"], K("W_out"), pref + "w2")

    def rcfn(e):
        ins = None
        for g, w in enumerate((2, 4, 8, 16)):
            ins = e.memset(rc[:, g, :], 1.0 / w)
        return ins
    P.op("pool", rcfn, writes=[K("rc")])

    def rcfn2(e):
        ins = None
        for g, w in enumerate((2, 4, 8, 16)):
            for t in range(w - 1):
                ins = e.memset(rc[:, g, t:t + 1], 1.0 / (t + 1))
        return ins
    P.op("pool", rcfn2, reads=[K("rc")], writes=[K("rc")])

    rot_T = Rot([0])
    rot_proj = Rot([1, 2, 3])
    rot_grp = Rot([4, 5])
    rot_out = Rot([6, 7])
    rot_v = Rot(list(range(NV)))
    rot_hn = Rot([0, 1])

    tiles = [(0, 16)] + [(16 + T * i, T) for i in range(ntiles)]

    def load_x(ti):
        r0, nt = tiles[ti]
        s = ti % 2
        if nt == 16:
            P.op("sp", lambda e: e.dma_start(out=xt[s][0:16, 0, :], in_=a["xin"][r0:r0 + 16, :]),
                 writes=[K("xt", s)], dma_slot=pref + f"x{s}")
        else:
            P.op("sp", lambda e: e.dma_start(out=xt[s][:, :, :], in_=a["xin"][r0:r0 + nt, :].rearrange("(b p) d -> p b d", p=128)),
                 writes=[K("xt", s)], dma_slot=pref + f"x{s}")

    hs_of = {}

    def norm_part(ti):
        r0, nt = tiles[ti]
        s = ti % 2
        nb = max(1, nt // 128)
        bl = min(nt, 128)
        for b in range(nb):
            P.op("act", lambda e, b=b: e.activation(junk[0:bl, :], xt[s][0:bl, b, :], AF.Square, accum_out=ss[0:bl, b:b + 1]),
                 reads=[K("xt", s)], writes=[K("junk"), K("ss")])
        rms_rstd(P, ss[0:bl, 0:nb], rs[0:bl, 0:nb], D, [K("ss")], [K("rs")])
        hs_of[ti] = []
        for b in range(nb):
            hs = rot_hn.next()
            hs_of[ti].append(hs)
            P.op("dve", lambda e, b=b, hs=hs: e.scalar_tensor_tensor(hn[hs][0:bl, :], xt[s][0:bl, b, :], rs[0:bl, b:b + 1], wt0[0:bl, :],
                                                                     ALU.mult, ALU.mult),
                 reads=[K("xt", s), K("rs"), K("wt0")], writes=[K("hn", hs)])

    def tr_part(ti):
        r0, nt = tiles[ti]
        s = ti % 2
        nb = max(1, nt // 128)
        bl = min(nt, 128)
        for b in range(nb):
            hs = hs_of[ti][b]
            bank = rot_T.next()
            pst = ps[bank].bitcast(BF16)

            def tfn(e, hs=hs, pst=pst):
                ins = None
                for k in range(8):
                    ins = e.transpose(pst[:, k * 128:k * 128 + bl], hn[hs][0:bl, k * 128:(k + 1) * 128], ident[0:bl, 0:bl])
                return ins
            P.op("pe", tfn, reads=[K("hn", hs), pref + "ident"], writes=[("ps", bank)])
            P.op("act", lambda e, b=b, pst=pst: e.activation(hnT[s][:, :, b * 128:b * 128 + bl],
                                                             pst[:, :].rearrange("p (k t) -> p k t", t=128)[:, :, 0:bl], AF.Copy),
                 reads=[("ps", bank)], writes=[K("hnT", s)])

    def inproj(ti, g):
        r0, nt = tiles[ti]
        s = ti % 2
        gp = g % 2
        first = (nt == 16)
        wwin = (2, 4, 8, 16)[g]
        E = 16 + nt
        for pair in range(2):
            chains = []
            for jj in (2 * pair, 2 * pair + 1):
                j = g * 4 + jj
                bank = rot_proj.next()
                mm_group(P, ps[bank][:, 0:nt], [(W_in[:, k, j * 128:(j + 1) * 128], hnT[s][:, k, 0:nt]) for k in range(8)],
                         reads=[K("W_in"), K("hnT", s)], writes=[("ps", bank)])
                vs = rot_v.next()
                V, A, B = vb[vs], sA[vs], sB[vs]
                kV, kA, kB = K("vb", vs), K("sA", vs), K("sB", vs)
                kH = K("vbh", vs)
                P.op("pool", lambda e, V=V, j=j: e.tensor_copy(V[:, 0:16], halo[:, j, :]), reads=[K("halo", j)], writes=[kH])
                P.op("act", lambda e, V=V, bank=bank: e.activation(V[:, 16:16 + nt], ps[bank][:, 0:nt], AF.Copy),
                     reads=[("ps", bank)], writes=[kV])
                P.op("pool", lambda e, V=V, j=j: e.tensor_copy(halo[:, j, :], V[:, nt:nt + 16]), reads=[kV], writes=[K("halo", j)])
                ch = []
                ch.append((lambda e, V=V, A=A: e.tensor_tensor(A[:, 2:E], V[:, 2:E], V[:, 1:E - 1], ALU.add), [kV, kH], [kA]))
                src, ksrc = A, kA
                if g >= 1:
                    ch.append((lambda e, A=A, B=B: e.tensor_tensor(B[:, 4:E], A[:, 4:E], A[:, 2:E - 2], ALU.add), [kA], [kB]))
                    src, ksrc = B, kB
                if g >= 2:
                    ch.append((lambda e, A=A, B=B: e.tensor_tensor(A[:, 8:E], B[:, 8:E], B[:, 4:E - 4], ALU.add), [kB], [kA]))
                    src, ksrc = A, kA
                if g >= 3:
                    ch.append((lambda e, A=A, B=B: e.tensor_tensor(B[:, 16:E], A[:, 16:E], A[:, 8:E - 8], ALU.add), [kA], [kB]))
                    src, ksrc = B, kB
                if first:
                    ch.append((lambda e, src=src: e.tensor_tensor(src[:, 16:E], src[:, 16:E], rc[:, g, :], ALU.mult), [ksrc, K("rc")], [ksrc]))
                    ch.append((lambda e, src=src, V=V, jj=jj: e.tensor_tensor(u[gp][:, jj, 0:nt], src[:, 16:E], V[:, 16:E], ALU.subtract),
                               [ksrc, kV], [K("u", gp)]))
                else:
                    ch.append((lambda e, src=src, V=V, jj=jj: e.scalar_tensor_tensor(u[gp][:, jj, 0:nt], src[:, 16:E], 1.0 / wwin, V[:, 16:E],
                                                                                   ALU.mult, ALU.subtract), [ksrc, kV], [K("u", gp)]))
                chains.append(ch)
            for step in range(len(chains[0])):
                for ch in chains:
                    fn, rd, wr = ch[step]
                    P.op("dve", fn, reads=rd, writes=wr)
        for jj in range(4):
            j = g * 4 + jj
            bank = rot_proj.next()
            mm_group(P, ps[bank][:, 0:nt], [(W_in[:, k, 2048 + j * 128:2048 + (j + 1) * 128], hnT[s][:, k, 0:nt]) for k in range(8)],
                     reads=[K("W_in"), K("hnT", s)], writes=[("ps", bank)])
            P.op("act", lambda e, bank=bank, jj=jj: e.activation(sg[gp][:, jj, 0:nt], ps[bank][:, 0:nt], AF.Silu),
                 reads=[("ps", bank)], writes=[K("sg", gp)])

    def grp(ti, g):
        r0, nt = tiles[ti]
        gp = g % 2
        for jj in range(4):
            j = g * 4 + jj
            bank = rot_grp.next()
            mm_group(P, ps[bank][:, 0:nt], [(W_grp[:, g * 4 + kk, jj * 128:(jj + 1) * 128], u[gp][:, kk, 0:nt]) for kk in range(4)],
                     reads=[K("W_grp"), K("u", gp)], writes=[("ps", bank)])
            P.op("dve", lambda e, bank=bank, j=j, jj=jj: e.scalar_tensor_tensor(m0[:, j, 0:nt], ps[bank][:, 0:nt], scale[:, j:j + 1],
                                                                              sg[gp][:, jj, 0:nt], ALU.mult, ALU.mult),
                 reads=[("ps", bank), K("scale"), K("sg", gp)], writes=[K("m0")])

    def outproj(ti):
        r0, nt = tiles[ti]
        s = ti % 2
        nb = max(1, nt // 128)
        bl = min(nt, 128)
        for b in range(nb):
            for hf in range(2):
                bank = rot_out.next()
                mm_group(P, ps[bank][0:bl, :], [(m0[:, k, b * 128:b * 128 + bl], W_out[:, k, hf * 512:(hf + 1) * 512]) for k in range(16)],
                         reads=[K("m0"), K("W_out")], writes=[("ps", bank)])
                P.op("dve", lambda e, b=b, hf=hf, bank=bank: e.tensor_tensor(xt[s][0:bl, b, hf * 512:(hf + 1) * 512],
                                                                           xt[s][0:bl, b, hf * 512:(hf + 1) * 512], ps[bank][0:bl, :], ALU.add),
                     reads=[("ps", bank), K("xt", s)], writes=[K("xt", s)])
            P.op("act", lambda e, b=b: e.activation(junk[0:bl, :], xt[s][0:bl, b, :], AF.Square, accum_out=ss[0:bl, 4 + b:5 + b]),
                 reads=[K("xt", s)], writes=[K("junk"), K("ss1")])
        rms_rstd(P, ss[0:bl, 4:4 + nb], rs[0:bl, 4:4 + nb], D, [K("ss1")], [K("rs1")])
        for b in range(nb):
            P.op("dve", lambda e, b=b: e.scalar_tensor_tensor(hn1[s][0:bl, b, :], xt[s][0:bl, b, :], rs[0:bl, 4 + b:5 + b], wt1[0:bl, :],
                                                              ALU.mult, ALU.mult),
                 reads=[K("xt", s), K("rs1"), K("wt1")], writes=[K("hn1", 0)])
        if nt == 16:
            P.op("sp", lambda e: e.dma_start(out=a["h1"][r0:r0 + 16, :], in_=xt[s][0:16, 0, :]), reads=[K("xt", s)], dma_slot=pref + f"oh{s}")
            P.op("sp", lambda e: e.dma_start(out=a["hn1"][r0:r0 + 16, :], in_=hn1[s][0:16, 0, :]), reads=[K("hn1", 0)],
                 writes=[("d_hn1loc", "meta")], dma_slot=pref + "on")
        else:
            P.op("sp", lambda e: e.dma_start(out=a["h1"][r0:r0 + nt, :].rearrange("(b p) d -> p b d", p=128), in_=xt[s][:, :, :]),
                 reads=[K("xt", s)], dma_slot=pref + f"oh{s}")
            P.op("sp", lambda e: e.dma_start(out=a["hn1"][r0:r0 + nt, :].rearrange("(b p) d -> p b d", p=128), in_=hn1[s][:, :, :]),
                 reads=[K("hn1", 0)], writes=[("d_hn1loc", (r0 - 16) // 1024)], dma_slot=pref + "on")

    nT = len(tiles)
    load_x(0)
    norm_part(0)
    tr_part(0)
    for ti in range(nT):
        if ti + 1 < nT:
            load_x(ti + 1)
        inproj(ti, 0)
        for g in range(4):
            if g + 1 < 4:
                inproj(ti, g + 1)
            grp(ti, g)
            if g == 1 and ti + 1 < nT:
                norm_part(ti + 1)
        if ti + 1 < nT:
            tr_part(ti + 1)
        outproj(ti)
        if after_tile is not None:
            after_tile(ti)


def build_p2(nc, P, ps, a, ntiles, T=512, pref="p2", after_unit=None):
    sb = SB(nc, pref)
    K = lambda *k: (pref,) + k
    HG = 4
    W2K = [K("W2", k) for k in range(8)]
    W2 = sb.t("W2", [128, 8, 4096], BF16)
    ident = make_ident(P, sb, "ident")
    ones = sb.t("ones", [128, 128], BF16)
    smask = sb.t("smask", [128, T], F32)
    emask = sb.t("emask", [128, T], F32)
    emask16 = sb.t("emask16", [128, 16], F32)
    rdec = [sb.t(f"rdec{i}", [128, HG, 8], F32) for i in range(2)]
    trim = sb.t("trim", [128, 64], F32)
    trim2 = sb.t("trim2", [128, 64], F32)
    lg = sb.t("lg", [128, 2, 8], F32)
    lb = sb.t("lb", [128, 8], F32)
    homl = sb.t("homl", [128, 8], F32)
    nhoml = sb.t("nhoml", [128, 8], F32)
    hb = sb.t("hb", [128, 8], F32)
    onw = sb.t("onw", [128, 8], F32)
    hn1 = [sb.t("hn1_0", [128, 4, D], BF16)] * 2
    hnT = [sb.t(f"hnT{i}", [128, 8, T], BF16) for i in range(2)]
    sig = [sb.t(f"sig{i}", [128, T], F32) for i in range(2)]
    gb = [sb.t(f"gb{i}", [128, T], F32) for i in range(2)]
    bb = [sb.t(f"bb{i}", [128, T], F32) for i in range(2)]
    qs = [sb.t(f"qs{i}", [128, T], F32) for i in range(2)]
    khf = [sb.t(f"khf{i}", [128, T], BF16) for i in range(2)]
    qt = [sb.t(f"qt{i}", [128, HG, T], BF16) for i in range(2)]
    kt = [sb.t(f"kt{i}", [128, HG, T], BF16) for i in range(2)]
    kh = [sb.t(f"kh{i}", [128, 4, HG, 128], BF16) for i in range(2)]
    vt = [sb.t(f"vt{i}", [128, 4, HG, 128], BF16) for i in range(2)]
    sgt = [sb.t(f"sgt{i}", [128, HG, T], BF16) for i in range(2)]
    dec = [sb.t(f"dec{i}", [128, HG, 8], F32) for i in range(2)]
    S = sb.t("S", [128, 8, 128], F32)
    Sb = sb.t("Sb", [128, 8, 128], BF16)
    scm = [sb.t(f"scm{i}", [128, HG, 64], BF16) for i in range(2)]
    osb = sb.t("osb", [128, HG, T], F32)
    osq = [sb.t(f"osq{i}", [128, T], BF16) for i in range(2)]
    rsd = [sb.t(f"rsd{i}", [128, T], F32) for i in range(2)]
    mo = [sb.t(f"mo{i}", [128, HG, T], BF16) for i in range(2)]

    if a.get("w2b") is not None:
        for k in range(8):
            P.op("sp" if k % 2 == 0 else "act", lambda e, k=k: e.dma_start(out=W2[:, k, :], in_=a["w2b"][k * 128:(k + 1) * 128, :]),
                 reads=[("d_w2b", 0)], writes=[K("W2", k)], dma_slot=pref + f"w{k % 2}")
    else:
        P.op("pool", lambda e: [e.dma_start(out=W2[:, k, :], in_=a["w2"][k * 128:(k + 1) * 128, :]) for k in range(8)],
             writes=W2K, dma_slot=pref + "w0", npieces=8)
    P.op("sp", lambda e: e.dma_start(out=lg[:], in_=a["lbl"].rearrange("l (h p) -> p l h", p=128), allow_slow_non_contiguous=True),
         writes=[K("lg")], dma_slot=pref + "c0")
    P.op("sp", lambda e: e.dma_start(out=onw[:], in_=a["onorm"].rearrange("(h p) -> p h", p=128), allow_slow_non_contiguous=True),
         writes=[K("onw")], dma_slot=pref + "c1")
    P.op("pool", lambda e: e.memset(ones[:], 1.0), writes=[K("ones")])
    P.op("pool", lambda e: e.memset(smask[:], 0.0), writes=[K("smask")])
    P.op("pool", lambda e: e.memset(smask[:].rearrange("p (c t) -> p c t", t=64)[:, :, 0:1], 1.0), reads=[K("smask")], writes=[K("smask")])
    P.op("pool", lambda e: e.memset(emask[:], 0.0), writes=[K("emask")])
    P.op("pool", lambda e: e.memset(emask[:].rearrange("p (c t) -> p c t", t=64)[:, :, 63:64], 1.0), reads=[K("emask")], writes=[K("emask")])
    P.op("pool", lambda e: e.memset(emask16[:], 0.0), writes=[K("emask")])
    P.op("pool", lambda e: e.memset(emask16[:, 15:16], 1.0), reads=[K("emask")], writes=[K("emask")])
    P.op("pool", lambda e: e.memset(trim[:], 1.0), writes=[K("trim")])
    P.op("pool", lambda e: e.memset(trim2[:], 1.0), writes=[K("trim2")])
    P.op("pool", lambda e: e.affine_select(trim[:], trim[:], [[1, 64]], ALU.is_ge, 0.0, base=0, channel_multiplier=-1),
         reads=[K("trim")], writes=[K("trim")])
    P.op("pool", lambda e: e.affine_select(trim2[:], trim2[:], [[1, 64]], ALU.is_ge, 0.0, base=64, channel_multiplier=-1),
         reads=[K("trim2")], writes=[K("trim2")])
    P.op("pool", lambda e: e.tensor_copy(trim[64:128, :], trim2[64:128, :]), reads=[K("trim"), K("trim2")], writes=[K("trim")])
    P.op("pool", lambda e: e.memset(S[:], 0.0), writes=[K("S", h) for h in range(8)])
    P.op("pool", lambda e: e.memset(Sb[:], 0.0), writes=[K("Sb", h) for h in range(8)])
    P.op("dve", lambda e: e.tensor_tensor(lb[:], lg[:, 1, :], lg[:, 0, :], ALU.subtract), reads=[K("lg")], writes=[K("lb")])
    P.op("act", lambda e: e.activation(lb[:], lb[:], AF.Sigmoid), reads=[K("lb")], writes=[K("lb")])
    P.op("dve", lambda e: e.tensor_scalar(homl[:], lb[:], -0.5, 0.5, ALU.mult, ALU.add), reads=[K("lb")], writes=[K("homl")])
    P.op("dve", lambda e: e.tensor_scalar(nhoml[:], lb[:], 0.5, -0.5, ALU.mult, ALU.add), reads=[K("lb")], writes=[K("nhoml")])
    P.op("dve", lambda e: e.tensor_scalar(hb[:], lb[:], 0.5, 0.5, ALU.mult, ALU.add), reads=[K("lb")], writes=[K("hb")])
    P.op("dve", lambda e: e.tensor_scalar(onw[:], onw[:], float(np.sqrt(128.0)), None, ALU.mult), reads=[K("onw")], writes=[K("onw")])
    kc = [K("lb"), K("homl"), K("nhoml"), K("hb")]

    rot_proj = Rot([1, 2, 4])
    rot_sc = Rot([3])
    rot_o = Rot([5, 6])
    rot_scm = Rot([0, 1])
    rot_m = Rot([0, 1])

    tiles = [(0, 16)] + [(16 + T * i, T) for i in range(ntiles)]

    def load_dma(ti):
        r0, nt = tiles[ti]
        s = ti % 2
        if nt == 16:
            P.op("sp", lambda e: e.dma_start(out=hn1[s][0:16, 0, :], in_=a["hn_src"](ti)), reads=[a["hn_key"](ti)], writes=[K("hn1", 0)], dma_slot=pref + f"x{s}")
        else:
            P.op("sp", lambda e: e.dma_start(out=hn1[s][:, :, :], in_=a["hn_src"](ti).rearrange("(b p) d -> p b d", p=128)),
                 reads=[a["hn_key"](ti)], writes=[K("hn1", 0)], dma_slot=pref + f"x{s}")

    def load_T_block(ti, b):
        r0, nt = tiles[ti]
        s = ti % 2
        bl = min(nt, 128)
        pst = ps[0].bitcast(BF16)

        def tfn(e):
            ins = None
            for k in range(8):
                ins = e.transpose(pst[:, k * 128:k * 128 + bl], hn1[s][0:bl, b, k * 128:(k + 1) * 128], ident[0:bl, 0:bl])
            return ins
        P.op("pe", tfn, reads=[K("hn1", 0), pref + "ident"], writes=[("ps", 0)])
        P.op("act", lambda e: e.activation(hnT[s][:, :, b * 128:b * 128 + bl],
                                           pst[:, :].rearrange("p (k t) -> p k t", t=128)[:, :, 0:bl], AF.Copy),
             reads=[("ps", 0)], writes=[K("hnT", s)])

    def nblocks(ti):
        return max(1, tiles[ti][1] // 128)

    def stageA(ti, hg, up):
        r0, nt = tiles[ti]
        s = ti % 2
        nb = max(1, nt // 128)
        bl = min(nt, 128)
        nch = max(1, nt // 64)
        cl = min(nt, 64)
        c3 = lambda ap: ap.rearrange("p (c t) -> p c t", t=cl)
        rev = lambda t_: bass.AP(t_, nt - 1, [[T, 128], [-1, nt]])
        rev_em = bass.AP(emask16, 15, [[16, 128], [-1, 16]]) if nt == 16 else rev(emask)
        nxt = list(range(nblocks(ti + 1))) if (hg == 1 and ti + 1 < len(tiles)) else []

        def next_block():
            if nxt:
                b = nxt.pop(0)
                load_T_block(ti + 1, b)
                if not nxt and ti + 2 < len(tiles):
                    load_dma(ti + 2)
        for b in range(nb):
            bank = rot_proj.next()
            c0 = 2048 + hg * 512
            mm_group(P, ps[bank][0:bl, :], [(hnT[s][:, k, b * 128:b * 128 + bl], W2[:, k, c0:c0 + 512]) for k in range(8)],
                     reads=W2K + [K("hnT", s)], writes=[("ps", bank)])
            P.op("act", lambda e, b=b, bank=bank: e.activation(vt[up][0:bl, b, :, :], ps[bank][0:bl, :].rearrange("p (h v) -> p h v", v=128), AF.Copy),
                 reads=[("ps", bank)], writes=[K("vt", up)])
            if b % 2 == 1:
                next_block()
                next_block()
            yield
        for pair in range(2):
            hp = [(2 * pair + x, hg * HG + 2 * pair + x, x) for x in range(2)]
            for hh, h, st in hp:
                bank = rot_proj.next()
                mm_group(P, ps[bank][:, 0:nt], [(W2[:, k, 1024 + h * 128:1024 + (h + 1) * 128], hnT[s][:, k, 0:nt]) for k in range(8)],
                         reads=W2K + [K("hnT", s)], writes=[("ps", bank)])
                P.op("act", lambda e, bank=bank, st=st: e.activation(sig[st][:, 0:nt], ps[bank][:, 0:nt], AF.Tanh, scale=0.5),
                     reads=[("ps", bank)], writes=[K("sig", st)])
            for hh, h, st in hp:
                P.op("act", lambda e, st=st, h=h: e.activation(gb[st][:, 0:nt], sig[st][:, 0:nt], AF.Identity, bias=hb[:, h:h + 1], scale=homl[:, h:h + 1]),
                     reads=[K("sig", st)] + kc, writes=[K("gb", st)])
            for hh, h, st in hp:
                P.op("act", lambda e, st=st, h=h: e.activation(sig[st][:, 0:nt], sig[st][:, 0:nt], AF.Identity, bias=homl[:, h:h + 1], scale=nhoml[:, h:h + 1]),
                     reads=[K("sig", st)] + kc, writes=[K("sig", st)])
            next_block()
            yield
            for hh, h, st in hp:
                bank = rot_proj.next()
                mm_group(P, ps[bank][:, 0:nt], [(W2[:, k, h * 128:(h + 1) * 128], hnT[s][:, k, 0:nt]) for k in range(8)],
                         reads=W2K + [K("hnT", s)], writes=[("ps", bank)])
                P.op("act", lambda e, bank=bank, st=st: e.activation(qs[st][:, 0:nt], ps[bank][:, 0:nt], AF.Silu),
                     reads=[("ps", bank)], writes=[K("qs", st)])
            for hh, h, st in hp:
                P.op("dve", lambda e, st=st: e.tensor_tensor_scan(bb[st][:, 0:nt], smask[:, 0:nt], gb[st][:, 0:nt], 1.0, ALU.max, ALU.mult),
                     reads=[K("gb", st), K("smask")], writes=[K("bb", st)])
            for hh, h, st in hp:
                P.op("dve", lambda e, st=st: e.tensor_tensor_scan(rev(gb[st]), rev_em, rev(gb[st]), 1.0, ALU.max, ALU.mult),
                     reads=[K("gb", st), K("emask")], writes=[K("gb", st)])
            next_block()
            yield
            if nt != 16:
                for hh, h, st in hp:
                    bank = rot_proj.next()
                    mm_group(P, ps[bank][:, 0:nt], [(W2[:, k, 3072 + h * 128:3072 + (h + 1) * 128], hnT[s][:, k, 0:nt]) for k in range(8)],
                             reads=W2K + [K("hnT", s)], writes=[("ps", bank)])
                    P.op("act", lambda e, bank=bank, hh=hh: e.activation(sgt[up][:, hh, 0:nt], ps[bank][:, 0:nt], AF.Silu),
                         reads=[("ps", bank)], writes=[K("sgt", up)])
            for hh, h, st in hp:
                P.op("pool", lambda e, st=st, hh=hh: e.tensor_copy(dec[up][:, hh, 0:nch], c3(bb[st][:, 0:nt])[:, :, cl - 1]),
                     reads=[K("bb", st)], writes=[K("dec", up)])
            for hh, h, st in hp:
                P.op("pool", lambda e, st=st: e.tensor_tensor(c3(sig[st][:, 0:nt])[:, :, 0:cl - 1], c3(sig[st][:, 0:nt])[:, :, 0:cl - 1],
                                                              c3(gb[st][:, 0:nt])[:, :, 1:cl], ALU.mult),
                     reads=[K("sig", st), K("gb", st)], writes=[K("sig", st)])
            for hh, h, st in hp:
                P.op("pool", lambda e, st=st, hh=hh: e.tensor_scalar(rdec[up][:, hh, 0:nch], c3(bb[st][:, 0:nt])[:, :, cl - 1], 1e-30, None, ALU.max),
                     reads=[K("bb", st)], writes=[K("rdec", up)])
            P.op("dve", lambda e, pair=pair: e.reciprocal(rdec[up][:, 2 * pair:2 * pair + 2, 0:nch], rdec[up][:, 2 * pair:2 * pair + 2, 0:nch]),
                 reads=[K("rdec", up)], writes=[K("rdec", up)])
            for hh, h, st in hp:
                P.op("pool", lambda e, st=st, hh=hh: e.tensor_tensor(qt[up][:, hh, 0:nt], qs[st][:, 0:nt], bb[st][:, 0:nt], ALU.mult),
                     reads=[K("qs", st), K("bb", st)], writes=[K("qt", up)])
            for hh, h, st in hp:
                P.op("act", lambda e, st=st: e.activation(khf[st][:, 0:nt], sig[st][:, 0:nt], AF.Copy), reads=[K("sig", st)], writes=[K("khf", st)])
            next_block()
            yield
            for hh, h, st in hp:
                P.op("pool", lambda e, st=st, hh=hh: e.tensor_tensor(c3(kt[up][:, hh, 0:nt]), c3(sig[st][:, 0:nt]),
                                                                     rdec[up][:, hh, 0:nch].unsqueeze(2).to_broadcast([128, nch, cl]), ALU.mult),
                     reads=[K("sig", st), K("rdec", up)], writes=[K("kt", up)])
            pst = ps[0].bitcast(BF16)

            def tfn(e, hp=hp):
                ins = None
                for x, (hh, h, st) in enumerate(hp):
                    for b in range(nb):
                        ins = e.transpose(pst[0:bl, (x * 4 + b) * 128:(x * 4 + b + 1) * 128], khf[st][:, b * 128:b * 128 + bl], ident[:, :])
                return ins
            P.op("pe", tfn, reads=[K("khf", 0), K("khf", 1), pref + "ident"], writes=[("ps", 0)])
            P.op("act", lambda e, pair=pair: e.activation(
                kh[up][0:bl, 0:nb, 2 * pair:2 * pair + 2, :].rearrange("p b x k -> p x b k"),
                pst[0:bl, :].rearrange("p (x b k) -> p x b k", x=2, k=128)[:, :, 0:nb, :], AF.Copy),
                reads=[("ps", 0)], writes=[K("kh", up)])
            next_block()
            yield
        while nxt:
            next_block()
            yield

    def stageBC(ti, hg, up):
        r0, nt = tiles[ti]
        nch = max(1, nt // 64)
        cl = min(nt, 64)
        hs = [hg * HG + hh for hh in range(HG)]
        for c in range(nch):
            blk, half = c // 2, c % 2
            p0 = half * 64
            cs = slice(c * 64, c * 64 + cl)
            if nt == 16:
                yield
            if nt != 16:
                bsc = rot_sc.next()

                def scfn(e, bsc=bsc, p0=p0, cs=cs):
                    ins = None
                    for hh in range(HG):
                        ins = e.matmul(ps[bsc][p0:p0 + cl, hh * 64:hh * 64 + cl], kt[up][:, hh, cs], qt[up][:, hh, cs], start=True, stop=True)
                    return ins
                P.op("pe", scfn, reads=[K("kt", up), K("qt", up)], writes=[("ps", bsc)])
                ms = rot_scm.next()
                P.op("dve", lambda e, bsc=bsc, ms=ms, p0=p0: e.tensor_tensor(
                    scm[ms][p0:p0 + cl, :, 0:cl],
                    ps[bsc][p0:p0 + cl, 0:HG * 64].rearrange("p (h t) -> p h t", t=64)[:, :, 0:cl],
                    trim[p0:p0 + cl, 0:cl].unsqueeze(1).to_broadcast([cl, HG, cl]), ALU.mult),
                    reads=[("ps", bsc), K("trim")], writes=[K("scm", ms)])
                yield
                bo = rot_o.next()

                def ofn(e, bo=bo, ms=ms, p0=p0, cs=cs, blk=blk):
                    ins = None
                    for hh in range(HG):
                        o_ap = ps[bo][:, hh * 64:hh * 64 + cl]
                        e.matmul(o_ap, vt[up][p0:p0 + cl, blk, hh, :], scm[ms][p0:p0 + cl, hh, 0:cl], start=True, stop=False)
                        ins = e.matmul(o_ap, Sb[:, hs[hh], :], qt[up][:, hh, cs], start=False, stop=True)
                    return ins
                P.op("pe", ofn, reads=[K("vt", up), K("scm", ms), K("qt", up)] + [K("Sb", h) for h in hs], writes=[("ps", bo)])
                P.op("act", lambda e, bo=bo, cs=cs: e.activation(osb[:, :, cs], ps[bo][:, 0:HG * 64].rearrange("p (h t) -> p h t", t=64)[:, :, 0:cl], AF.Copy),
                     reads=[("ps", bo)], writes=[K("osb")])

            def sfn(e, p0=p0, blk=blk):
                ins = None
                for hh in range(HG):
                    ins = e.matmul(ps[7][:, hh * 128:(hh + 1) * 128], kh[up][p0:p0 + cl, blk, hh, :], vt[up][p0:p0 + cl, blk, hh, :], start=True, stop=True)
                return ins
            P.op("pe", sfn, reads=[K("kh", up), K("vt", up)], writes=[("ps", 7)])
            for hh in range(HG):
                h = hs[hh]
                P.op("dve", lambda e, h=h, hh=hh, c=c: e.scalar_tensor_tensor(S[:, h, :], S[:, h, :], dec[up][:, hh, c:c + 1],
                                                                             ps[7][:, hh * 128:(hh + 1) * 128], ALU.mult, ALU.add),
                     reads=[("ps", 7), K("S", h), K("dec", up)], writes=[K("S", h)])
            P.op("act", lambda e: e.activation(Sb[:, hs[0]:hs[0] + HG, :], S[:, hs[0]:hs[0] + HG, :], AF.Copy),
                 reads=[K("S", h) for h in hs], writes=[K("Sb", h) for h in hs])
            yield
        if nt == 16:
            return
        mp = rot_m.next()
        hps = [[(2 * pair + x, hs[2 * pair + x], x) for x in range(2)] for pair in range(2)]
        banks = {}

        def c_square(hp):
            for hh, h, st in hp:
                P.op("act", lambda e, hh=hh, st=st: e.activation(osq[st][:, :], osb[:, hh, :], AF.Square), reads=[K("osb")], writes=[K("osq", st)])

        def c_norm(hp):
            for hh, h, st in hp:
                bank = rot_proj.next()
                banks[st] = bank
                P.op("pe", lambda e, st=st, bank=bank: e.matmul(ps[bank][:, :], ones[:, :], osq[st][:, :], start=True, stop=True),
                     reads=[K("ones"), K("osq", st)], writes=[("ps", bank)])
            for hh, h, st in hp:
                P.op("act", lambda e, st=st, bank=banks[st]: e.activation(rsd[st][:, :], ps[bank][:, :], AF.Ln, bias=128.0 * EPS),
                     reads=[("ps", banks[st])], writes=[K("rsd", st)])
            for hh, h, st in hp:
                P.op("act", lambda e, st=st: e.activation(rsd[st][:, :], rsd[st][:, :], AF.Exp, scale=-0.5),
                     reads=[K("rsd", st)], writes=[K("rsd", st)])

        def c_out(hp):
            for hh, h, st in hp:
                P.op("dve", lambda e, hh=hh, h=h, st=st: e.scalar_tensor_tensor(rsd[st][:, :], osb[:, hh, :], onw[:, h:h + 1], rsd[st][:, :], ALU.mult, ALU.mult),
                     reads=[K("osb"), K("onw"), K("rsd", st)], writes=[K("rsd", st)])
            for hh, h, st in hp:
                P.op("dve", lambda e, hh=hh, st=st: e.tensor_tensor(mo[mp][:, hh, :], rsd[st][:, :], sgt[up][:, hh, :], ALU.mult),
                     reads=[K("rsd", st), K("sgt", up)], writes=[K("mo", mp)])
        c_square(hps[0])
        yield
        c_norm(hps[0])
        yield
        c_out(hps[0])
        c_square(hps[1])
        yield
        c_norm(hps[1])
        yield
        c_out(hps[1])
        P.op("sp", lambda e: e.dma_start(out=a["m_dst"](ti, hg), in_=mo[mp][:, :, :]),
             reads=[K("mo", mp)], writes=[a["m_key"](ti)], dma_slot=pref + f"om{mp}")
        if after_unit is not None:
            after_unit(ti, hg)
        yield

    units = [(ti, hg) for ti in range(len(tiles)) for hg in range(2)]
    load_dma(0)
    for b in range(nblocks(0)):
        load_T_block(0, b)
    if len(tiles) > 1:
        load_dma(1)
    prevBC = None
    for n, (ti, hg) in enumerate(units):
        gA = stageA(ti, hg, n % 2)
        gB = prevBC
        aliveA, aliveB = True, gB is not None

        def step(g):
            try:
                next(g)
                return True
            except StopIteration:
                return False
        if aliveB:
            aliveA = step(gA)
        while aliveA or aliveB:
            if aliveB:
                aliveB = step(gB)
            if aliveA:
                aliveA = step(gA)
            if aliveB:
                aliveB = step(gB)
        prevBC = stageBC(ti, hg, n % 2)
    for _ in prevBC:
        pass


def build_p3(nc, P, ps, a, ntiles, T=512, pref="p3"):
    sb = SB(nc, pref)
    K = lambda *k: (pref,) + k
    NB = T // 128
    WOK = [K("W_out", k) for k in range(16)]
    W_out = sb.t("W_out", [128, 16, 1024], BF16)
    wtf = sb.t("wtf", [128, D], F32)
    mt = [sb.t(f"mt{i}", [128, 16, T], BF16) for i in range(2)]
    ht = [sb.t(f"ht{i}", [128, NB, D], F32) for i in range(2)]
    ot = [sb.t(f"ot{i}", [128, NB, D], F32) for i in range(2)]
    junk = sb.t("junk", [128, D], BF16)
    ss = sb.t("ss", [128, 8], F32)
    rs = sb.t("rs", [128, 8], F32)
    if a.get("wob") is not None:
        for k in range(16):
            P.op("sp" if k % 2 == 0 else "act", lambda e, k=k: e.dma_start(out=W_out[:, k, :], in_=a["wob"][k * 128:(k + 1) * 128, :]),
                 reads=[("d_wob", 0)], writes=[K("W_out", k)], dma_slot=pref + f"w{k % 2}")
    else:
        P.op("pool", lambda e: [e.dma_start(out=W_out[:, k, :], in_=a["w_out"][k * 128:(k + 1) * 128, :]) for k in range(16)],
             writes=WOK, dma_slot=pref + "w0", npieces=16)
    P.op("sp", lambda e: e.dma_start(out=wtf[:], in_=a["nwf"].partition_broadcast(128)), writes=[K("wtf")], dma_slot=pref + "c0")
    P.op("dve", lambda e: e.tensor_scalar(wtf[:], wtf[:], float(np.sqrt(float(D))), None, ALU.mult), reads=[K("wtf")], writes=[K("wtf")])
    rot_out = Rot([0, 1, 2, 3])

    def load(i):
        s = i % 2
        P.op("sp", lambda e: [e.dma_start(out=mt[s][:, k0:k1, :], in_=src) for (k0, k1, src) in a["m_src"](e, i)],
             reads=list(a["m_keys"](i)), writes=[K("mt", s)], dma_slot=pref + f"m{s}", npieces=a["m_npieces"])
        P.op("sp", lambda e: e.dma_start(out=ht[s][:, :, :], in_=a["h1"][16 + i * T:16 + (i + 1) * T, :].rearrange("(b p) d -> p b d", p=128)),
             writes=[K("ht", s)], dma_slot=pref + f"h{s}")

    def compute(i):
        s = i % 2
        for b in range(NB):
            for hf in range(2):
                bank = rot_out.next()
                mm_group(P, ps[bank][:, :], [(mt[s][:, k, b * 128:(b + 1) * 128], W_out[:, k, hf * 512:(hf + 1) * 512]) for k in range(16)],
                         reads=[K("mt", s)] + WOK, writes=[("ps", bank)])
                P.op("dve", lambda e, b=b, hf=hf, bank=bank: e.tensor_tensor(ht[s][:, b, hf * 512:(hf + 1) * 512],
                                                                           ht[s][:, b, hf * 512:(hf + 1) * 512], ps[bank][:, :], ALU.add),
                     reads=[("ps", bank), K("ht", s)], writes=[K("ht", s)])
            P.op("act", lambda e, b=b: e.activation(junk[:, :], ht[s][:, b, :], AF.Square, accum_out=ss[:, b:b + 1]),
                 reads=[K("ht", s)], writes=[K("junk"), K("ss")])
        rms_rstd(P, ss[:, 0:NB], rs[:, 0:NB], D, [K("ss")], [K("rs")])
        for b in range(NB):
            P.op("dve", lambda e, b=b: e.scalar_tensor_tensor(ot[s][:, b, :], ht[s][:, b, :], rs[:, b:b + 1], wtf[:, :], ALU.mult, ALU.mult),
                 reads=[K("ht", s), K("rs"), K("wtf")], writes=[K("ot", s)])
        P.op("sp", lambda e: e.dma_start(out=a["out"][i * T:(i + 1) * T, :].rearrange("(b p) d -> p b d", p=128), in_=ot[s][:, :, :]),
             reads=[K("ot", s)], dma_slot=pref + f"o{s}")

    load(0)
    for i in range(ntiles):
        if i + 1 < ntiles:
            load(i + 1)
        compute(i)


def _psum(nc, es):
    return [es.enter_context(nc.psum_tensor(f"psb{i}", [128, 512], F32)) for i in range(8)]


def make_p1(ntiles=16, T=256):
    nc = bass.Bass("TRN2", target_bir_lowering=False, dynamic_dma_scratch_size=SCRATCH)
    dt = lambda n, s, d, k: nc.dram_tensor(n, s, d, kind=k).ap()
    a = dict(
        xin=dt("xin", [LTOK, D], F32, "ExternalInput"),
        nw0=dt("nw0", [D], F32, "ExternalInput"), nw1=dt("nw1", [D], F32, "ExternalInput"),
        w_in=dt("w_in", [D, 4096], F32, "ExternalInput"), w_grp=dt("w_grp", [4, 512, 512], F32, "ExternalInput"),
        scale=dt("scale", [2048], F32, "ExternalInput"), w_out=dt("w_out", [2048, D], F32, "ExternalInput"),
        h1=dt("h1", [LTOK, D], F32, "ExternalOutput"), hn1=dt("hn1", [LTOK, D], BF16, "ExternalOutput"),
    )
    with ExitStack() as es:
        ps = _psum(nc, es)
        P = Prog(nc)
        build_p1(nc, P, ps, a, ntiles, T)
        P.finish(es)
    return nc


def make_p2(ntiles=16, T=512):
    nc = bass.Bass("TRN2", target_bir_lowering=False, dynamic_dma_scratch_size=SCRATCH)
    dt = lambda n, s, d, k: nc.dram_tensor(n, s, d, kind=k).ap()
    a = dict(
        hn1=dt("hn1", [LFULL, D], BF16, "ExternalInput"),
        w2=dt("w2", [D, 4096], F32, "ExternalInput"),
        lbl=dt("lbl", [2, 1024], F32, "ExternalInput"),
        onorm=dt("onorm", [1024], F32, "ExternalInput"),
        m=dt("m", [8, 128, SEQ], BF16, "ExternalOutput"),
    )
    a["hn_src"] = lambda ti: a["hn1"][0:16, :] if ti == 0 else a["hn1"][16 + T * (ti - 1):16 + T * ti, :]
    a["hn_key"] = lambda ti: ("d_hn1all", 0)
    a["m_dst"] = lambda ti, hg: a["m"][hg * 4:(hg + 1) * 4, :, T * (ti - 1):T * ti].rearrange("h p t -> p h t")
    a["m_key"] = lambda ti: ("d_mloc", 0)
    with ExitStack() as es:
        ps = _psum(nc, es)
        P = Prog(nc)
        build_p2(nc, P, ps, a, ntiles, T)
        P.finish(es)
    return nc


def make_p3(ntiles=8, T=512):
    nc = bass.Bass("TRN2", target_bir_lowering=False, dynamic_dma_scratch_size=SCRATCH)
    dt = lambda n, s, d, k: nc.dram_tensor(n, s, d, kind=k).ap()
    a = dict(
        m=dt("m", [16, 128, HALF], BF16, "ExternalInput"),
        h1=dt("h1", [LTOK, D], F32, "ExternalInput"),
        w_out=dt("w_out", [2048, D], F32, "ExternalInput"),
        nwf=dt("nwf", [D], F32, "ExternalInput"),
        out=dt("out", [HALF, D], F32, "ExternalOutput"),
    )
    a["m_src"] = lambda e, i: [(0, 16, a["m"][:, :, i * T:(i + 1) * T].rearrange("h p t -> p h t"))]
    a["m_npieces"] = 1
    a["m_keys"] = lambda i: [("d_mall", 0)]
    with ExitStack() as es:
        ps = _psum(nc, es)
        P = Prog(nc)
        build_p3(nc, P, ps, a, ntiles, T)
        P.finish(es)
    return nc


GROUPS = [[0, 1], [2, 3], [4, 5], [6, 7]]


def make_fused(nt1=16, nt2=16, nt3=8, T1=256, T2=512, T3=512):
    nc = bass.Bass("TRN2", target_bir_lowering=False, dynamic_dma_scratch_size=SCRATCH)
    dt = lambda n, s, d, k="Internal": nc.dram_tensor(n, s, d, kind=k).ap()
    EI, EO = "ExternalInput", "ExternalOutput"
    rows = [1024, 1024, 1024, 1024]
    row0 = [NMETA, NMETA + 1024, NMETA + 2048, NMETA + 3072]
    hn1meta = dt("hn1meta", [2 * NMETA, D], BF16)
    h1 = dt("h1", [LTOK, D], F32)
    hn1loc = dt("hn1loc", [LTOK, D], BF16)
    hn1all = [dt(f"hn1all{j}", [2 * rows[j], D], BF16) for j in range(4)]
    mloc = dt("mloc", [2, 4, 128, 2, 8, T2], BF16)
    mall = dt("mall", [2, 4, 2, 128, 2, 8, T2], BF16)
    a1 = dict(
        xin=dt("xin", [LTOK, D], F32, EI), nw0=dt("nw0", [D], F32, EI), nw1=dt("nw1", [D], F32, EI),
        w_in=dt("w_in", [D, 4096], F32, EI), w_grp=dt("w_grp", [4, 512, 512], F32, EI),
        scale=dt("scale", [2048], F32, EI), w_out=dt("w_out0", [2048, D], F32, EI),
        h1=h1, hn1=hn1loc)
    a2 = dict(w2=dt("w2", [D, 4096], F32, EI), lbl=dt("lbl", [2, 1024], F32, EI), onorm=dt("onorm", [1024], F32, EI))
    a3 = dict(h1=h1, w_out=dt("w_out1", [2048, D], F32, EI), nwf=dt("nwf", [D], F32, EI), out=dt("out", [HALF, D], F32, EO))
    a2["w2b"] = dt("w2b", [D, 4096], BF16)
    a3["wob"] = dt("wob", [2048, D], BF16)

    def hn_loc(ti):
        if ti == 0:
            return "meta", 0
        i = ti - 1
        rank, li = i // 8, T2 * (i % 8)
        j = li // 1024
        return j, rank * rows[j] + li - 1024 * j
    def hn_src(ti):
        j, r = hn_loc(ti)
        if ti == 0:
            return hn1meta[0:NMETA, :]
        return hn1all[j][r:r + T2, :]
    a2["hn_src"] = hn_src
    a2["hn_key"] = lambda ti: ("d_hn1all", hn_loc(ti)[0])
    def m_piece(ti):
        i = ti - 1
        return i // 8, (i % 8) // 2, i % 2
    def m_dst(ti, hg):
        half, q, w = m_piece(ti)
        return mloc[half, q, :, w, hg * 4:(hg + 1) * 4, :]
    a2["m_dst"] = m_dst
    a2["m_key"] = lambda ti: ("d_mloc", m_piece(ti)[0] * 4 + m_piece(ti)[1])
    par_cache = {}

    def m_src(e, i):
        if "p" not in par_cache:
            par_cache["p"] = e.snap(e.partition_id() % 2)
        par = par_cache["p"]
        q, w = i // 2, i % 2
        return [(8 * r, 8 * r + 8, mall[bass.ds(par, 1), q, r, :, w, :, :].rearrange("a p h t -> p (a h) t")) for r in range(2)]
    a3["m_src"] = m_src
    a3["m_npieces"] = 2
    a3["m_keys"] = lambda i: [("d_mall", 0, i // 2), ("d_mall", 1, i // 2)]

    with ExitStack() as es:
        ps = _psum(nc, es)
        P = Prog(nc)
        pend = []

        def flush():
            while pend:
                pend.pop(0)()

        def gather_h(j):
            P.op("pool", lambda e: e.collective_compute("AllGather", ALU.bypass, replica_groups=GROUPS,
                                                        ins=[hn1loc[row0[j]:row0[j] + rows[j], :]], outs=[hn1all[j]]),
                 reads=[("d_hn1loc", j)], writes=[("d_hn1all", j)], dma_slot=f"cch{j}", inc=1, exempt=True)

        def gather_meta():
            P.op("pool", lambda e: e.collective_compute("AllGather", ALU.bypass, replica_groups=GROUPS,
                                                        ins=[hn1loc[0:NMETA, :]], outs=[hn1meta]),
                 reads=[("d_hn1loc", "meta")], writes=[("d_hn1all", "meta")], dma_slot="cchm", inc=1, exempt=True)

        def after_tile(ti):
            flush()
            if 3 <= ti < 15:
                precast_piece(ti - 3)
            if ti == 0:
                pend.append(gather_meta)
            if ti >= 1 and ti % 4 == 0:
                j = ti // 4 - 1
                pend.append(lambda j=j: gather_h(j))
        def precast_piece(n):
            if n < 8:
                P.op("pool", lambda e: e.dma_start(out=a2["w2b"][n * 128:(n + 1) * 128, :], in_=a2["w2"][n * 128:(n + 1) * 128, :]),
                     writes=[("d_w2b", 0)], dma_slot="pc2", exempt=True)
            elif n < 12:
                k = n - 8
                P.op("pool", lambda e: e.dma_start(out=a3["wob"][k * 512:(k + 1) * 512, :], in_=a3["w_out"][k * 512:(k + 1) * 512, :]),
                     writes=[("d_wob", 0)], dma_slot="pc3", exempt=True)
        import os
        PH = os.environ.get("FUSE_PHASES", "123")
        NOG = os.environ.get("FUSE_NOGATHER", "0") == "1"
        if NOG:
            gather_h = lambda j: None
            gather_meta = lambda: None
        if "1" in PH:
            build_p1(nc, P, ps, a1, nt1, T1, after_tile=after_tile)
        flush()
        P.barrier()

        def gather_m(half, q):
            P.op("pool", lambda e: e.collective_compute("AllGather", ALU.bypass, replica_groups=GROUPS,
                                                        ins=[mloc[half, q].rearrange("p w h t -> p (w h t)")],
                                                        outs=[mall[half, q].rearrange("r p w h t -> (r p) (w h t)")]),
                 reads=[("d_mloc", half * 4 + q)], writes=[("d_mall", half, q)], dma_slot=f"ccm{half}{q}", inc=1, exempt=True)

        def after_unit(ti, hg):
            flush()
            i = ti - 1
            if hg == 1 and i % 2 == 1:
                half, q, _ = m_piece(ti)
                pend.append(lambda half=half, q=q: gather_m(half, q))
        if NOG:
            gather_m = lambda half, q: None
        if "2" in PH:
            build_p2(nc, P, ps, a2, nt2, T2, after_unit=after_unit)
        flush()
        P.barrier()
        if "3" in PH:
            build_p3(nc, P, ps, a3, nt3, T3)
        P.finish(es)
    return nc


def fused_inputs(x, meta_tokens, norm_w, pool_w_in, pool_w_grp, pool_scale, pool_w_out,
                 hgrn_w_in, hgrn_lb_logits, hgrn_o_norm, hgrn_w_out, final_norm_w):
    m1 = p1_inputs(x, meta_tokens, norm_w, pool_w_in, pool_w_grp, pool_scale, pool_w_out)
    maps = []
    for c in range(8):
        d = dict(m1[c])
        d["w_out0"] = d.pop("w_out")
        d.update(p2_weights(c, hgrn_w_in, hgrn_lb_logits, hgrn_o_norm))
        d["w_out1"] = hgrn_w_out[0]
        d["nwf"] = final_norm_w
        maps.append(d)
    return maps


def p1_inputs(x, meta_tokens, norm_w, pool_w_in, pool_w_grp, pool_scale, pool_w_out):
    maps = []
    for c in range(8):
        b, half = c // 2, c % 2
        if half == 0:
            xin = np.concatenate([meta_tokens, x[b, 0:HALF]], axis=0)
        else:
            xin = x[b, HALF - NMETA:SEQ]
        maps.append(dict(xin=np.ascontiguousarray(xin), nw0=norm_w[0], nw1=norm_w[1], w_in=pool_w_in[0], w_grp=pool_w_grp[0],
                         scale=pool_scale[0], w_out=pool_w_out[0]))
    return maps


def p2_weights(c, hgrn_w_in, hgrn_lb_logits, hgrn_o_norm):
    hg = c % 2
    w = hgrn_w_in[0]
    cols = [w[:, part * 2048 + hg * 1024: part * 2048 + (hg + 1) * 1024] for part in range(4)]
    return dict(w2=np.ascontiguousarray(np.concatenate(cols, axis=1)),
                lbl=np.ascontiguousarray(hgrn_lb_logits[:, hg * 1024:(hg + 1) * 1024]),
                onorm=np.ascontiguousarray(hgrn_o_norm[0, hg * 1024:(hg + 1) * 1024]))


def kernel_unfused(x, meta_tokens, norm_w, pool_w_in, pool_w_grp, pool_scale, pool_w_out,
                   hgrn_w_in, hgrn_lb_logits, hgrn_o_norm, hgrn_w_out, final_norm_w):
    f = lambda t: np.asarray(t, dtype=np.float32)
    x, meta_tokens, norm_w = f(x), f(meta_tokens), f(norm_w)
    cores = list(range(8))
    nc1 = make_p1()
    r1 = run_bass_kernel_spmd(nc1, p1_inputs(x, meta_tokens, norm_w, f(pool_w_in), f(pool_w_grp), f(pool_scale), f(pool_w_out)),
                              core_ids=cores).results
    nc2 = make_p2()
    maps2 = []
    for c in cores:
        b = c // 2
        hn_full = np.concatenate([r1[2 * b]["hn1"], r1[2 * b + 1]["hn1"][NMETA:]], axis=0)
        d = p2_weights(c, f(hgrn_w_in), f(hgrn_lb_logits), f(hgrn_o_norm))
        d["hn1"] = np.ascontiguousarray(hn_full)
        maps2.append(d)
    r2 = run_bass_kernel_spmd(nc2, maps2, core_ids=cores).results
    nc3 = make_p3()
    maps3 = []
    for c in cores:
        b, half = c // 2, c % 2
        m = np.concatenate([r2[2 * b]["m"][:, :, half * HALF:(half + 1) * HALF],
                            r2[2 * b + 1]["m"][:, :, half * HALF:(half + 1) * HALF]], axis=0)
        maps3.append(dict(m=np.ascontiguousarray(m), h1=r1[c]["h1"], w_out=f(hgrn_w_out)[0], nwf=f(final_norm_w)))
    r3 = run_bass_kernel_spmd(nc3, maps3, core_ids=cores).results
    out = np.empty((4, SEQ, D), np.float32)
    for c in cores:
        b, half = c // 2, c % 2
        out[b, half * HALF:(half + 1) * HALF] = r3[c]["out"]
    return out


def kernel(x, meta_tokens, norm_w, pool_w_in, pool_w_grp, pool_scale, pool_w_out,
           hgrn_w_in, hgrn_lb_logits, hgrn_o_norm, hgrn_w_out, final_norm_w):
    f = lambda t: np.ascontiguousarray(np.asarray(t, dtype=np.float32))
    maps = fused_inputs(f(x), f(meta_tokens), f(norm_w), f(pool_w_in), f(pool_w_grp), f(pool_scale), f(pool_w_out),
                        f(hgrn_w_in), f(hgrn_lb_logits), f(hgrn_o_norm), f(hgrn_w_out), f(final_norm_w))
    nc = make_fused()
    res = run_bass_kernel_spmd(nc, maps, core_ids=list(range(8))).results
    out = np.empty((4, SEQ, D), np.float32)
    for c in range(8):
        b, half = c // 2, c % 2
        out[b, half * HALF:(half + 1) * HALF] = res[c]["out"]
    return out
```

```python
from contextlib import ExitStack
import numpy as np
import ml_dtypes
import concourse.bass as bass
import concourse.mybir as mybir
from concourse.bass_utils import run_bass_kernel_spmd

F32 = mybir.dt.float32
BF16 = mybir.dt.bfloat16
AF = mybir.ActivationFunctionType
ALU = mybir.AluOpType

D = 1024
NMETA = 16
SEQ = 8192
HALF = SEQ // 2
EPS = 1e-6
LTOK = NMETA + HALF
LFULL = NMETA + SEQ

SCRATCH = 4096

ENGS = ("pe", "act", "dve", "pool", "sp")


class Prog:
    def __init__(self, nc):
        self.nc = nc
        self.ops = []
        self.phase = 0

    def barrier(self):
        self.phase += 1

    def op(self, eng, fn, reads=(), writes=(), dma_slot=None, npieces=1, inc=16, exempt=False):
        assert eng in ENGS
        self.ops.append(dict(eng=eng, fn=fn, reads=tuple(reads), writes=tuple(writes),
                             dma_slot=dma_slot, npieces=npieces, inc=inc, phase=self.phase, exempt=exempt))

    def finish(self, es):
        nc = self.nc
        ops = self.ops
        last_writer = {}
        readers = {}
        prev_last, cur_last, cur_phase = {}, {}, 0
        for i, o in enumerate(ops):
            if o["phase"] != cur_phase:
                prev_last.update(cur_last)
                cur_last, cur_phase = {}, o["phase"]
            deps = set(prev_last.values())
            if not o["exempt"]:
                cur_last[("dma", o["dma_slot"]) if o["dma_slot"] is not None else o["eng"]] = i
            for k in o["reads"]:
                if k in last_writer:
                    deps.add(last_writer[k])
            for k in o["writes"]:
                if k in last_writer:
                    deps.add(last_writer[k])
                for r in readers.get(k, ()):
                    deps.add(r)
            deps.discard(i)
            o["deps"] = deps
            for k in o["reads"]:
                readers.setdefault(k, []).append(i)
            for k in o["writes"]:
                last_writer[k] = i
                readers[k] = []

        def chan(o):
            return ("dma", o["dma_slot"]) if o["dma_slot"] is not None else o["eng"]

        for o in ops:
            o["marked"] = False
        for i, o in enumerate(ops):
            best = {}
            for d in o["deps"]:
                so = ops[d]
                c = chan(so)
                if so["dma_slot"] is None and so["eng"] == "pe" and o["eng"] == "pe" and o["dma_slot"] is None:
                    continue
                if c not in best or best[c] < d:
                    best[c] = d
            o["need"] = best
            for d in best.values():
                ops[d]["marked"] = True
        cnt = {}
        for o in ops:
            c = chan(o)
            if o["dma_slot"] is not None:
                cnt[c] = cnt.get(c, 0) + o["inc"] * o["npieces"]
                o["count"] = cnt[c]
            elif o["marked"]:
                cnt[c] = cnt.get(c, 0) + 1
                o["count"] = cnt[c]
        sems = {}
        for c in cnt:
            nm = "s_" + (c if isinstance(c, str) else "d_" + str(c[1]))
            sems[c] = es.enter_context(nc.semaphore(nm))
        engobj = {"pe": "tensor", "act": "scalar", "dve": "vector", "pool": "gpsimd", "sp": "sync"}
        with nc.Block() as block:
            for e in ENGS:
                my = [o for o in ops if o["eng"] == e]
                if not my:
                    continue

                def body(eng, my=my, e=e):
                    waited = {}
                    for o in my:
                        for c, d in sorted(o["need"].items(), key=lambda kv: str(kv[0])):
                            v = ops[d]["count"]
                            if waited.get(c, 0) < v:
                                eng.wait_ge(sems[c], v)
                                waited[c] = v
                        if o["dma_slot"] is not None:
                            insts = o["fn"](eng)
                            if not isinstance(insts, (list, tuple)):
                                insts = [insts]
                            assert len(insts) == o["npieces"], (len(insts), o["npieces"])
                            for ins in insts:
                                ins.then_inc(sems[("dma", o["dma_slot"])], o["inc"])
                        else:
                            ins = o["fn"](eng)
                            if o["marked"]:
                                ins.then_inc(sems[e], 1)
                    final = {}
                    for o in my:
                        if o["dma_slot"] is not None:
                            c = ("dma", o["dma_slot"])
                            final[c] = max(final.get(c, 0), o["count"])
                    for c, v in final.items():
                        if waited.get(c, 0) < v:
                            eng.wait_ge(sems[c], v)

                getattr(block, engobj[e])(body)


class SB:
    def __init__(self, nc, pref, base=SCRATCH, limit=192 * 1024):
        self.nc, self.pref, self.off, self.limit = nc, pref, base, limit

    def t(self, name, shape, dt):
        n = 1
        for s in shape[1:]:
            n *= s
        nbytes = n * mybir.dt.size(dt)
        nbytes = (nbytes + 63) // 64 * 64
        h = self.nc.alloc_sbuf_tensor_at(self.pref + name, list(shape), dt, offset=self.off)
        self.off += nbytes
        assert self.off <= self.limit, (self.pref, name, self.off)
        return h


class Rot:
    def __init__(self, items):
        self.items, self.i = list(items), 0

    def next(self):
        x = self.items[self.i % len(self.items)]
        self.i += 1
        return x


def mm_group(P, out_ap, pairs, reads, writes):
    def fn(e, out_ap=out_ap, pairs=pairs):
        n = len(pairs)
        ins = None
        for i, (l, r) in enumerate(pairs):
            ins = e.matmul(out_ap, l, r, start=(i == 0), stop=(i == n - 1))
        return ins
    P.op("pe", fn, reads=reads, writes=writes)


def load_weight_bf16(P, eng, dst, src, key, slot):
    kt = dst.shape[1]
    def fn(e):
        return [e.dma_start(out=dst[:, k, :], in_=src[k * 128:(k + 1) * 128, :]) for k in range(kt)]
    P.op(eng, fn, writes=[key], dma_slot=slot, npieces=kt)


def load_weight_cols(P, eng, dst, src, c0, c1, key, slot):
    kt = dst.shape[1]
    def fn(e):
        return [e.dma_start(out=dst[:, k, c0:c1], in_=src[k * 128:(k + 1) * 128, c0:c1]) for k in range(kt)]
    P.op(eng, fn, writes=[key], dma_slot=slot, npieces=kt)


def make_ident(P, sb, name):
    ident = sb.t(name, [128, 128], BF16)
    key = sb.pref + name
    P.op("pool", lambda e: e.memset(ident[:], 1.0), writes=[key])
    P.op("pool", lambda e: e.affine_select(ident[:], ident[:], [[-1, 128]], ALU.is_equal, 0.0,
                                           base=0, channel_multiplier=1), reads=[key], writes=[key])
    return ident


def rms_rstd(P, ss_ap, rs_ap, n, keys_r, keys_w):
    P.op("act", lambda e: e.activation(rs_ap, ss_ap, AF.Sqrt, bias=float(n) * EPS), reads=keys_r, writes=keys_w)
    P.op("dve", lambda e: e.reciprocal(rs_ap, rs_ap), reads=keys_w, writes=keys_w)


def build_p1(nc, P, ps, a, ntiles, T=256, pref="p1", after_tile=None):
    sb = SB(nc, pref)
    K = lambda *k: (pref,) + k
    NB = T // 128
    W_in = sb.t("W_in", [128, 8, 4096], BF16)
    W_grp = sb.t("W_grp", [128, 16, 512], BF16)
    W_out = sb.t("W_out", [128, 16, 1024], BF16)
    wt0 = sb.t("wt0", [128, D], F32)
    wt1 = sb.t("wt1", [128, D], F32)
    scale = sb.t("scale", [128, 16], F32)
    ident = make_ident(P, sb, "ident")
    xt = [sb.t(f"xt{i}", [128, NB, D], F32) for i in range(2)]
    hn = [sb.t(f"hn{i}", [128, D], BF16) for i in range(2)]
    hnT = [sb.t(f"hnT{i}", [128, 8, T], BF16) for i in range(2)]
    NV = 4
    vb = [sb.t(f"vb{i}", [128, 16 + T], F32) for i in range(NV)]
    sA = [sb.t(f"sA{i}", [128, 16 + T], F32) for i in range(NV)]
    sB = [sb.t(f"sB{i}", [128, 16 + T], F32) for i in range(NV)]
    halo = sb.t("halo", [128, 16, 16], F32)
    u = [sb.t(f"u{i}", [128, 4, T], BF16) for i in range(2)]
    sg = [sb.t(f"sg{i}", [128, 4, T], BF16) for i in range(2)]
    m0 = sb.t("m0", [128, 16, T], BF16)
    hn1 = [sb.t("hn1_0", [128, NB, D], BF16)] * 2
    junk = sb.t("junk", [128, D], BF16)
    ss = sb.t("ss", [128, 8], F32)
    rs = sb.t("rs", [128, 8], F32)
    rc = sb.t("rc", [128, 4, 16], F32)

    P.op("sp", lambda e: e.dma_start(out=wt0[:], in_=a["nw0"].partition_broadcast(128)), writes=[K("wt0")], dma_slot=pref + "c0")
    P.op("sp", lambda e: e.dma_start(out=wt1[:], in_=a["nw1"].partition_broadcast(128)), writes=[K("wt1")], dma_slot=pref + "c1")
    P.op("sp", lambda e: e.dma_start(out=scale[:], in_=a["scale"].rearrange("(j p) -> p j", p=128),
                                     allow_slow_non_contiguous=True), writes=[K("scale")], dma_slot=pref + "c2")
    sqD = float(np.sqrt(float(D)))
    P.op("dve", lambda e: e.tensor_scalar(wt0[:], wt0[:], sqD, None, ALU.mult), reads=[K("wt0")], writes=[K("wt0")])
    P.op("dve", lambda e: e.tensor_scalar(wt1[:], wt1[:], sqD, None, ALU.mult), reads=[K("wt1")], writes=[K("wt1")])
    P.op("pool", lambda e: e.memset(halo[:], 0.0), writes=[K("halo", j) for j in range(16)])
    load_weight_bf16(P, "pool", W_in, a["w_in"], K("W_in"), pref + "w0")
    P.op("pool", lambda e: [e.dma_start(out=W_grp[:, g * 4 + kk, :], in_=a["w_grp"][g, kk * 128:(kk + 1) * 128, :])
                            for g in range(4) for kk in range(4)], writes=[K("W_grp")], dma_slot=pref + "w1", npieces=16)
    load_weight_bf16(P, "pool", W_out, a["w_out"], K("W_out"), pref + "w2")

    def rcfn(e):
        ins = None
        for g, w in enumerate((2, 4, 8, 16)):
            ins = e.memset(rc[:, g, :], 1.0 / w)
        return ins
    P.op("pool", rcfn, writes=[K("rc")])

    def rcfn2(e):
        ins = None
        for g, w in enumerate((2, 4, 8, 16)):
            for t in range(w - 1):
                ins = e.memset(rc[:, g, t:t + 1], 1.0 / (t + 1))
        return ins
    P.op("pool", rcfn2, reads=[K("rc")], writes=[K("rc")])

    rot_T = Rot([0])
    rot_proj = Rot([1, 2, 3])
    rot_grp = Rot([4, 5, 6, 7])
    rot_out = Rot([6, 7])
    rot_v = Rot(list(range(NV)))
    rot_hn = Rot([0, 1])

    tiles = [(0, 16)] + [(16 + T * i, T) for i in range(ntiles)]

    def load_x(ti):
        r0, nt = tiles[ti]
        s = ti % 2
        if nt == 16:
            P.op("sp", lambda e: e.dma_start(out=xt[s][0:16, 0, :], in_=a["xin"][r0:r0 + 16, :]),
                 writes=[K("xt", s)], dma_slot=pref + f"x{s}")
        else:
            P.op("sp", lambda e: e.dma_start(out=xt[s][:, :, :], in_=a["xin"][r0:r0 + nt, :].rearrange("(b p) d -> p b d", p=128)),
                 writes=[K("xt", s)], dma_slot=pref + f"x{s}")

    hs_of = {}

    def norm_part(ti):
        r0, nt = tiles[ti]
        s = ti % 2
        nb = max(1, nt // 128)
        bl = min(nt, 128)
        for b in range(nb):
            P.op("act", lambda e, b=b: e.activation(junk[0:bl, :], xt[s][0:bl, b, :], AF.Square, accum_out=ss[0:bl, b:b + 1]),
                 reads=[K("xt", s)], writes=[K("junk"), K("ss")])
        rms_rstd(P, ss[0:bl, 0:nb], rs[0:bl, 0:nb], D, [K("ss")], [K("rs")])
        hs_of[ti] = []
        for b in range(nb):
            hs = rot_hn.next()
            hs_of[ti].append(hs)
            P.op("dve", lambda e, b=b, hs=hs: e.scalar_tensor_tensor(hn[hs][0:bl, :], xt[s][0:bl, b, :], rs[0:bl, b:b + 1], wt0[0:bl, :],
                                                                     ALU.mult, ALU.mult),
                 reads=[K("xt", s), K("rs"), K("wt0")], writes=[K("hn", hs)])

    def tr_part(ti):
        r0, nt = tiles[ti]
        s = ti % 2
        nb = max(1, nt // 128)
        bl = min(nt, 128)
        for b in range(nb):
            hs = hs_of[ti][b]
            bank = rot_T.next()
            pst = ps[bank].bitcast(BF16)

            def tfn(e, hs=hs, pst=pst):
                ins = None
                for k in range(8):
                    ins = e.transpose(pst[:, k * 128:k * 128 + bl], hn[hs][0:bl, k * 128:(k + 1) * 128], ident[0:bl, 0:bl])
                return ins
            P.op("pe", tfn, reads=[K("hn", hs), pref + "ident"], writes=[("ps", bank)])
            P.op("act", lambda e, b=b, pst=pst: e.activation(hnT[s][:, :, b * 128:b * 128 + bl],
                                                             pst[:, :].rearrange("p (k t) -> p k t", t=128)[:, :, 0:bl], AF.Copy),
                 reads=[("ps", bank)], writes=[K("hnT", s)])

    def inproj(ti, g):
        r0, nt = tiles[ti]
        s = ti % 2
        gp = g % 2
        first = (nt == 16)
        wwin = (2, 4, 8, 16)[g]
        E = 16 + nt
        for pair in range(2):
            chains = []
            for jj in (2 * pair, 2 * pair + 1):
                j = g * 4 + jj
                bank = rot_proj.next()
                mm_group(P, ps[bank][:, 0:nt], [(W_in[:, k, j * 128:(j + 1) * 128], hnT[s][:, k, 0:nt]) for k in range(8)],
                         reads=[K("W_in"), K("hnT", s)], writes=[("ps", bank)])
                vs = rot_v.next()
                V, A, B = vb[vs], sA[vs], sB[vs]
                kV, kA, kB = K("vb", vs), K("sA", vs), K("sB", vs)
                kH = K("vbh", vs)
                P.op("pool", lambda e, V=V, j=j: e.tensor_copy(V[:, 0:16], halo[:, j, :]), reads=[K("halo", j)], writes=[kH])
                P.op("act", lambda e, V=V, bank=bank: e.activation(V[:, 16:16 + nt], ps[bank][:, 0:nt], AF.Copy),
                     reads=[("ps", bank)], writes=[kV])
                P.op("pool", lambda e, V=V, j=j: e.tensor_copy(halo[:, j, :], V[:, nt:nt + 16]), reads=[kV], writes=[K("halo", j)])
                ch = []
                ch.append((lambda e, V=V, A=A: e.tensor_tensor(A[:, 2:E], V[:, 2:E], V[:, 1:E - 1], ALU.add), [kV, kH], [kA]))
                src, ksrc = A, kA
                if g >= 1:
                    ch.append((lambda e, A=A, B=B: e.tensor_tensor(B[:, 4:E], A[:, 4:E], A[:, 2:E - 2], ALU.add), [kA], [kB]))
                    src, ksrc = B, kB
                if g >= 2:
                    ch.append((lambda e, A=A, B=B: e.tensor_tensor(A[:, 8:E], B[:, 8:E], B[:, 4:E - 4], ALU.add), [kB], [kA]))
                    src, ksrc = A, kA
                if g >= 3:
                    ch.append((lambda e, A=A, B=B: e.tensor_tensor(B[:, 16:E], A[:, 16:E], A[:, 8:E - 8], ALU.add), [kA], [kB]))
                    src, ksrc = B, kB
                if first:
                    ch.append((lambda e, src=src: e.tensor_tensor(src[:, 16:E], src[:, 16:E], rc[:, g, :], ALU.mult), [ksrc, K("rc")], [ksrc]))
                    ch.append((lambda e, src=src, V=V, jj=jj: e.tensor_tensor(u[gp][:, jj, 0:nt], src[:, 16:E], V[:, 16:E], ALU.subtract),
                               [ksrc, kV], [K("u", gp)]))
                else:
                    ch.append((lambda e, src=src, V=V, jj=jj: e.scalar_tensor_tensor(u[gp][:, jj, 0:nt], src[:, 16:E], 1.0 / wwin, V[:, 16:E],
                                                                                   ALU.mult, ALU.subtract), [ksrc, kV], [K("u", gp)]))
                chains.append(ch)
            for step in range(len(chains[0])):
                for ch in chains:
                    fn, rd, wr = ch[step]
                    P.op("dve", fn, reads=rd, writes=wr)
        for jj in range(4):
            j = g * 4 + jj
            bank = rot_proj.next()
            mm_group(P, ps[bank][:, 0:nt], [(W_in[:, k, 2048 + j * 128:2048 + (j + 1) * 128], hnT[s][:, k, 0:nt]) for k in range(8)],
                     reads=[K("W_in"), K("hnT", s)], writes=[("ps", bank)])
            P.op("act", lambda e, bank=bank, jj=jj: e.activation(sg[gp][:, jj, 0:nt], ps[bank][:, 0:nt], AF.Silu),
                 reads=[("ps", bank)], writes=[K("sg", gp)])

    def grp(ti, g):
        r0, nt = tiles[ti]
        gp = g % 2
        for jj in range(4):
            j = g * 4 + jj
            bank = rot_grp.next()
            mm_group(P, ps[bank][:, 0:nt], [(W_grp[:, g * 4 + kk, jj * 128:(jj + 1) * 128], u[gp][:, kk, 0:nt]) for kk in range(4)],
                     reads=[K("W_grp"), K("u", gp)], writes=[("ps", bank)])
            P.op("dve", lambda e, bank=bank, j=j, jj=jj: e.scalar_tensor_tensor(m0[:, j, 0:nt], ps[bank][:, 0:nt], scale[:, j:j + 1],
                                                                              sg[gp][:, jj, 0:nt], ALU.mult, ALU.mult),
                 reads=[("ps", bank), K("scale"), K("sg", gp)], writes=[K("m0")])

    def outproj(ti):
        r0, nt = tiles[ti]
        s = ti % 2
        nb = max(1, nt // 128)
        bl = min(nt, 128)
        for b in range(nb):
            for hf in range(2):
                bank = rot_out.next()
                mm_group(P, ps[bank][0:bl, :], [(m0[:, k, b * 128:b * 128 + bl], W_out[:, k, hf * 512:(hf + 1) * 512]) for k in range(16)],
                         reads=[K("m0"), K("W_out")], writes=[("ps", bank)])
                P.op("dve", lambda e, b=b, hf=hf, bank=bank: e.tensor_tensor(xt[s][0:bl, b, hf * 512:(hf + 1) * 512],
                                                                           xt[s][0:bl, b, hf * 512:(hf + 1) * 512], ps[bank][0:bl, :], ALU.add),
                     reads=[("ps", bank), K("xt", s)], writes=[K("xt", s)])
            P.op("act", lambda e, b=b: e.activation(junk[0:bl, :], xt[s][0:bl, b, :], AF.Square, accum_out=ss[0:bl, 4 + b:5 + b]),
                 reads=[K("xt", s)], writes=[K("junk"), K("ss1")])
        rms_rstd(P, ss[0:bl, 4:4 + nb], rs[0:bl, 4:4 + nb], D, [K("ss1")], [K("rs1")])
        for b in range(nb):
            P.op("dve", lambda e, b=b: e.scalar_tensor_tensor(hn1[s][0:bl, b, :], xt[s][0:bl, b, :], rs[0:bl, 4 + b:5 + b], wt1[0:bl, :],
                                                              ALU.mult, ALU.mult),
                 reads=[K("xt", s), K("rs1"), K("wt1")], writes=[K("hn1", 0)])
        if nt == 16:
            P.op("sp", lambda e: e.dma_start(out=a["h1"][r0:r0 + 16, :], in_=xt[s][0:16, 0, :]), reads=[K("xt", s)], dma_slot=pref + f"oh{s}")
            P.op("sp", lambda e: e.dma_start(out=a["hn1"][r0:r0 + 16, :], in_=hn1[s][0:16, 0, :]), reads=[K("hn1", 0)],
                 writes=[("d_hn1loc", "meta")], dma_slot=pref + "on")
        else:
            P.op("sp", lambda e: e.dma_start(out=a["h1"][r0:r0 + nt, :].rearrange("(b p) d -> p b d", p=128), in_=xt[s][:, :, :]),
                 reads=[K("xt", s)], dma_slot=pref + f"oh{s}")
            P.op("sp", lambda e: e.dma_start(out=a["hn1"][r0:r0 + nt, :].rearrange("(b p) d -> p b d", p=128), in_=hn1[s][:, :, :]),
                 reads=[K("hn1", 0)], writes=[("d_hn1loc", (r0 - 16) // 1024)], dma_slot=pref + "on")

    nT = len(tiles)
    load_x(0)
    norm_part(0)
    tr_part(0)
    for ti in range(nT):
        if ti + 1 < nT:
            load_x(ti + 1)
        inproj(ti, 0)
        for g in range(4):
            if g + 1 < 4:
                inproj(ti, g + 1)
            grp(ti, g)
            if g == 1 and ti + 1 < nT:
                norm_part(ti + 1)
        if ti + 1 < nT:
            tr_part(ti + 1)
        outproj(ti)
        if after_tile is not None:
            after_tile(ti)


def build_p2(nc, P, ps, a, ntiles, T=512, pref="p2", after_unit=None):
    sb = SB(nc, pref)
    K = lambda *k: (pref,) + k
    HG = 4
    W2K = [K("W2", k) for k in range(8)]
    W2 = sb.t("W2", [128, 8, 4096], BF16)
    ident = make_ident(P, sb, "ident")
    ones = sb.t("ones", [128, 128], BF16)
    smask = sb.t("smask", [128, T], F32)
    emask = sb.t("emask", [128, T], F32)
    emask16 = sb.t("emask16", [128, 16], F32)
    rdec = [sb.t(f"rdec{i}", [128, HG, 8], F32) for i in range(2)]
    trim = sb.t("trim", [128, 64], F32)
    trim2 = sb.t("trim2", [128, 64], F32)
    lg = sb.t("lg", [128, 2, 8], F32)
    lb = sb.t("lb", [128, 8], F32)
    homl = sb.t("homl", [128, 8], F32)
    nhoml = sb.t("nhoml", [128, 8], F32)
    hb = sb.t("hb", [128, 8], F32)
    onw = sb.t("onw", [128, 8], F32)
    hn1 = [sb.t("hn1_0", [128, 4, D], BF16)] * 2
    hnT = [sb.t(f"hnT{i}", [128, 8, T], BF16) for i in range(2)]
    sig = [sb.t(f"sig{i}", [128, T], F32) for i in range(2)]
    gb = [sb.t(f"gb{i}", [128, T], F32) for i in range(2)]
    bb = [sb.t(f"bb{i}", [128, T], F32) for i in range(2)]
    qs = [sb.t(f"qs{i}", [128, T], F32) for i in range(2)]
    khf = [sb.t(f"khf{i}", [128, T], BF16) for i in range(2)]
    qt = [sb.t(f"qt{i}", [128, HG, T], BF16) for i in range(2)]
    kt = [sb.t(f"kt{i}", [128, HG, T], BF16) for i in range(2)]
    kh = [sb.t(f"kh{i}", [128, 4, HG, 128], BF16) for i in range(2)]
    vt = [sb.t(f"vt{i}", [128, 4, HG, 128], BF16) for i in range(2)]
    sgt = [sb.t(f"sgt{i}", [128, HG, T], BF16) for i in range(2)]
    dec = [sb.t(f"dec{i}", [128, HG, 8], F32) for i in range(2)]
    S = sb.t("S", [128, 8, 128], F32)
    Sb = sb.t("Sb", [128, 8, 128], BF16)
    scm = [sb.t(f"scm{i}", [128, HG, 64], BF16) for i in range(2)]
    osb = sb.t("osb", [128, HG, T], F32)
    osq = [sb.t(f"osq{i}", [128, T], BF16) for i in range(2)]
    rsd = [sb.t(f"rsd{i}", [128, T], F32) for i in range(2)]
    mo = [sb.t(f"mo{i}", [128, HG, T], BF16) for i in range(2)]

    if a.get("w2b") is not None:
        for k in range(8):
            P.op("sp" if k % 2 == 0 else "act", lambda e, k=k: e.dma_start(out=W2[:, k, :], in_=a["w2b"][k * 128:(k + 1) * 128, :]),
                 reads=[("d_w2b", 0)], writes=[K("W2", k)], dma_slot=pref + f"w{k % 2}")
    else:
        P.op("pool", lambda e: [e.dma_start(out=W2[:, k, :], in_=a["w2"][k * 128:(k + 1) * 128, :]) for k in range(8)],
             writes=W2K, dma_slot=pref + "w0", npieces=8)
    P.op("sp", lambda e: e.dma_start(out=lg[:], in_=a["lbl"].rearrange("l (h p) -> p l h", p=128), allow_slow_non_contiguous=True),
         writes=[K("lg")], dma_slot=pref + "c0")
    P.op("sp", lambda e: e.dma_start(out=onw[:], in_=a["onorm"].rearrange("(h p) -> p h", p=128), allow_slow_non_contiguous=True),
         writes=[K("onw")], dma_slot=pref + "c1")
    P.op("pool", lambda e: e.memset(ones[:], 1.0), writes=[K("ones")])
    P.op("pool", lambda e: e.memset(smask[:], 0.0), writes=[K("smask")])
    P.op("pool", lambda e: e.memset(smask[:].rearrange("p (c t) -> p c t", t=64)[:, :, 0:1], 1.0), reads=[K("smask")], writes=[K("smask")])
    P.op("pool", lambda e: e.memset(emask[:], 0.0), writes=[K("emask")])
    P.op("pool", lambda e: e.memset(emask[:].rearrange("p (c t) -> p c t", t=64)[:, :, 63:64], 1.0), reads=[K("emask")], writes=[K("emask")])
    P.op("pool", lambda e: e.memset(emask16[:], 0.0), writes=[K("emask")])
    P.op("pool", lambda e: e.memset(emask16[:, 15:16], 1.0), reads=[K("emask")], writes=[K("emask")])
    P.op("pool", lambda e: e.memset(trim[:], 1.0), writes=[K("trim")])
    P.op("pool", lambda e: e.memset(trim2[:], 1.0), writes=[K("trim2")])
    P.op("pool", lambda e: e.affine_select(trim[:], trim[:], [[1, 64]], ALU.is_ge, 0.0, base=0, channel_multiplier=-1),
         reads=[K("trim")], writes=[K("trim")])
    P.op("pool", lambda e: e.affine_select(trim2[:], trim2[:], [[1, 64]], ALU.is_ge, 0.0, base=64, channel_multiplier=-1),
         reads=[K("trim2")], writes=[K("trim2")])
    P.op("pool", lambda e: e.tensor_copy(trim[64:128, :], trim2[64:128, :]), reads=[K("trim"), K("trim2")], writes=[K("trim")])
    P.op("pool", lambda e: e.memset(S[:], 0.0), writes=[K("S", h) for h in range(8)])
    P.op("pool", lambda e: e.memset(Sb[:], 0.0), writes=[K("Sb", h) for h in range(8)])
    P.op("dve", lambda e: e.tensor_tensor(lb[:], lg[:, 1, :], lg[:, 0, :], ALU.subtract), reads=[K("lg")], writes=[K("lb")])
    P.op("act", lambda e: e.activation(lb[:], lb[:], AF.Sigmoid), reads=[K("lb")], writes=[K("lb")])
    P.op("dve", lambda e: e.tensor_scalar(homl[:], lb[:], -0.5, 0.5, ALU.mult, ALU.add), reads=[K("lb")], writes=[K("homl")])
    P.op("dve", lambda e: e.tensor_scalar(nhoml[:], lb[:], 0.5, -0.5, ALU.mult, ALU.add), reads=[K("lb")], writes=[K("nhoml")])
    P.op("dve", lambda e: e.tensor_scalar(hb[:], lb[:], 0.5, 0.5, ALU.mult, ALU.add), reads=[K("lb")], writes=[K("hb")])
    P.op("dve", lambda e: e.tensor_scalar(onw[:], onw[:], float(np.sqrt(128.0)), None, ALU.mult), reads=[K("onw")], writes=[K("onw")])
    kc = [K("lb"), K("homl"), K("nhoml"), K("hb")]

    rot_proj = Rot([1, 2, 4])
    rot_sc = Rot([3])
    rot_o = Rot([5, 6])
    rot_scm = Rot([0, 1])
    rot_m = Rot([0, 1])

    tiles = [(0, 16)] + [(16 + T * i, T) for i in range(ntiles)]

    def load_dma(ti):
        r0, nt = tiles[ti]
        s = ti % 2
        if nt == 16:
            P.op("sp", lambda e: e.dma_start(out=hn1[s][0:16, 0, :], in_=a["hn_src"](ti)), reads=[a["hn_key"](ti)], writes=[K("hn1", 0)], dma_slot=pref + f"x{s}")
        else:
            P.op("sp", lambda e: e.dma_start(out=hn1[s][:, :, :], in_=a["hn_src"](ti).rearrange("(b p) d -> p b d", p=128)),
                 reads=[a["hn_key"](ti)], writes=[K("hn1", 0)], dma_slot=pref + f"x{s}")

    def load_T_block(ti, b):
        r0, nt = tiles[ti]
        s = ti % 2
        bl = min(nt, 128)
        pst = ps[0].bitcast(BF16)

        def tfn(e):
            ins = None
            for k in range(8):
                ins = e.transpose(pst[:, k * 128:k * 128 + bl], hn1[s][0:bl, b, k * 128:(k + 1) * 128], ident[0:bl, 0:bl])
            return ins
        P.op("pe", tfn, reads=[K("hn1", 0), pref + "ident"], writes=[("ps", 0)])
        P.op("act", lambda e: e.activation(hnT[s][:, :, b * 128:b * 128 + bl],
                                           pst[:, :].rearrange("p (k t) -> p k t", t=128)[:, :, 0:bl], AF.Copy),
             reads=[("ps", 0)], writes=[K("hnT", s)])

    def nblocks(ti):
        return max(1, tiles[ti][1] // 128)

    def stageA(ti, hg, up):
        r0, nt = tiles[ti]
        s = ti % 2
        nb = max(1, nt // 128)
        bl = min(nt, 128)
        nch = max(1, nt // 64)
        cl = min(nt, 64)
        c3 = lambda ap: ap.rearrange("p (c t) -> p c t", t=cl)
        rev = lambda t_: bass.AP(t_, nt - 1, [[T, 128], [-1, nt]])
        rev_em = bass.AP(emask16, 15, [[16, 128], [-1, 16]]) if nt == 16 else rev(emask)
        nxt = list(range(nblocks(ti + 1))) if (hg == 1 and ti + 1 < len(tiles)) else []

        def next_block():
            if nxt:
                b = nxt.pop(0)
                load_T_block(ti + 1, b)
                if not nxt and ti + 2 < len(tiles):
                    load_dma(ti + 2)
        for b in range(nb):
            bank = rot_proj.next()
            c0 = 2048 + hg * 512
            mm_group(P, ps[bank][0:bl, :], [(hnT[s][:, k, b * 128:b * 128 + bl], W2[:, k, c0:c0 + 512]) for k in range(8)],
                     reads=W2K + [K("hnT", s)], writes=[("ps", bank)])
            P.op("act", lambda e, b=b, bank=bank: e.activation(vt[up][0:bl, b, :, :], ps[bank][0:bl, :].rearrange("p (h v) -> p h v", v=128), AF.Copy),
                 reads=[("ps", bank)], writes=[K("vt", up)])
            if b % 2 == 1:
                next_block()
                next_block()
            yield
        for pair in range(2):
            hp = [(2 * pair + x, hg * HG + 2 * pair + x, x) for x in range(2)]
            for hh, h, st in hp:
                bank = rot_proj.next()
                mm_group(P, ps[bank][:, 0:nt], [(W2[:, k, 1024 + h * 128:1024 + (h + 1) * 128], hnT[s][:, k, 0:nt]) for k in range(8)],
                         reads=W2K + [K("hnT", s)], writes=[("ps", bank)])
                P.op("act", lambda e, bank=bank, st=st: e.activation(sig[st][:, 0:nt], ps[bank][:, 0:nt], AF.Tanh, scale=0.5),
                     reads=[("ps", bank)], writes=[K("sig", st)])
            for hh, h, st in hp:
                P.op("act", lambda e, st=st, h=h: e.activation(gb[st][:, 0:nt], sig[st][:, 0:nt], AF.Identity, bias=hb[:, h:h + 1], scale=homl[:, h:h + 1]),
                     reads=[K("sig", st)] + kc, writes=[K("gb", st)])
            for hh, h, st in hp:
                P.op("act", lambda e, st=st, h=h: e.activation(sig[st][:, 0:nt], sig[st][:, 0:nt], AF.Identity, bias=homl[:, h:h + 1], scale=nhoml[:, h:h + 1]),
                     reads=[K("sig", st)] + kc, writes=[K("sig", st)])
            next_block()
            yield
            for hh, h, st in hp:
                bank = rot_proj.next()
                mm_group(P, ps[bank][:, 0:nt], [(W2[:, k, h * 128:(h + 1) * 128], hnT[s][:, k, 0:nt]) for k in range(8)],
                         reads=W2K + [K("hnT", s)], writes=[("ps", bank)])
                P.op("act", lambda e, bank=bank, st=st: e.activation(qs[st][:, 0:nt], ps[bank][:, 0:nt], AF.Silu),
                     reads=[("ps", bank)], writes=[K("qs", st)])
            for hh, h, st in hp:
                P.op("dve", lambda e, st=st: e.tensor_tensor_scan(bb[st][:, 0:nt], smask[:, 0:nt], gb[st][:, 0:nt], 1.0, ALU.max, ALU.mult),
                     reads=[K("gb", st), K("smask")], writes=[K("bb", st)])
            for hh, h, st in hp:
                P.op("dve", lambda e, st=st: e.tensor_tensor_scan(rev(gb[st]), rev_em, rev(gb[st]), 1.0, ALU.max, ALU.mult),
                     reads=[K("gb", st), K("emask")], writes=[K("gb", st)])
            next_block()
            yield
            if nt != 16:
                for hh, h, st in hp:
                    bank = rot_proj.next()
                    mm_group(P, ps[bank][:, 0:nt], [(W2[:, k, 3072 + h * 128:3072 + (h + 1) * 128], hnT[s][:, k, 0:nt]) for k in range(8)],
                             reads=W2K + [K("hnT", s)], writes=[("ps", bank)])
                    P.op("act", lambda e, bank=bank, hh=hh: e.activation(sgt[up][:, hh, 0:nt], ps[bank][:, 0:nt], AF.Silu),
                         reads=[("ps", bank)], writes=[K("sgt", up)])
            for hh, h, st in hp:
                P.op("pool", lambda e, st=st, hh=hh: e.tensor_copy(dec[up][:, hh, 0:nch], c3(bb[st][:, 0:nt])[:, :, cl - 1]),
                     reads=[K("bb", st)], writes=[K("dec", up)])
            for hh, h, st in hp:
                P.op("pool", lambda e, st=st: e.tensor_tensor(c3(sig[st][:, 0:nt])[:, :, 0:cl - 1], c3(sig[st][:, 0:nt])[:, :, 0:cl - 1],
                                                              c3(gb[st][:, 0:nt])[:, :, 1:cl], ALU.mult),
                     reads=[K("sig", st), K("gb", st)], writes=[K("sig", st)])
            for hh, h, st in hp:
                P.op("pool", lambda e, st=st, hh=hh: e.tensor_scalar(rdec[up][:, hh, 0:nch], c3(bb[st][:, 0:nt])[:, :, cl - 1], 1e-30, None, ALU.max),
                     reads=[K("bb", st)], writes=[K("rdec", up)])
            P.op("dve", lambda e, pair=pair: e.reciprocal(rdec[up][:, 2 * pair:2 * pair + 2, 0:nch], rdec[up][:, 2 * pair:2 * pair + 2, 0:nch]),
                 reads=[K("rdec", up)], writes=[K("rdec", up)])
            for hh, h, st in hp:
                P.op("pool", lambda e, st=st, hh=hh: e.tensor_tensor(qt[up][:, hh, 0:nt], qs[st][:, 0:nt], bb[st][:, 0:nt], ALU.mult),
                     reads=[K("qs", st), K("bb", st)], writes=[K("qt", up)])
            for hh, h, st in hp:
                P.op("act", lambda e, st=st: e.activation(khf[st][:, 0:nt], sig[st][:, 0:nt], AF.Copy), reads=[K("sig", st)], writes=[K("khf", st)])
            next_block()
            yield
            for hh, h, st in hp:
                P.op("pool", lambda e, st=st, hh=hh: e.tensor_tensor(c3(kt[up][:, hh, 0:nt]), c3(sig[st][:, 0:nt]),
                                                                     rdec[up][:, hh, 0:nch].unsqueeze(2).to_broadcast([128, nch, cl]), ALU.mult),
                     reads=[K("sig", st), K("rdec", up)], writes=[K("kt", up)])
            pst = ps[0].bitcast(BF16)

            def tfn(e, hp=hp):
                ins = None
                for x, (hh, h, st) in enumerate(hp):
                    for b in range(nb):
                        ins = e.transpose(pst[0:bl, (x * 4 + b) * 128:(x * 4 + b + 1) * 128], khf[st][:, b * 128:b * 128 + bl], ident[:, :])
                return ins
            P.op("pe", tfn, reads=[K("khf", 0), K("khf", 1), pref + "ident"], writes=[("ps", 0)])
            P.op("act", lambda e, pair=pair: e.activation(
                kh[up][0:bl, 0:nb, 2 * pair:2 * pair + 2, :].rearrange("p b x k -> p x b k"),
                pst[0:bl, :].rearrange("p (x b k) -> p x b k", x=2, k=128)[:, :, 0:nb, :], AF.Copy),
                reads=[("ps", 0)], writes=[K("kh", up)])
            next_block()
            yield
        while nxt:
            next_block()
            yield

    def stageBC(ti, hg, up):
        r0, nt = tiles[ti]
        nch = max(1, nt // 64)
        cl = min(nt, 64)
        hs = [hg * HG + hh for hh in range(HG)]
        for c in range(nch):
            blk, half = c // 2, c % 2
            p0 = half * 64
            cs = slice(c * 64, c * 64 + cl)
            if nt == 16:
                yield
            if nt != 16:
                bsc = rot_sc.next()

                def scfn(e, bsc=bsc, p0=p0, cs=cs):
                    ins = None
                    for hh in range(HG):
                        ins = e.matmul(ps[bsc][p0:p0 + cl, hh * 64:hh * 64 + cl], kt[up][:, hh, cs], qt[up][:, hh, cs], start=True, stop=True)
                    return ins
                P.op("pe", scfn, reads=[K("kt", up), K("qt", up)], writes=[("ps", bsc)])
                ms = rot_scm.next()
                P.op("dve", lambda e, bsc=bsc, ms=ms, p0=p0: e.tensor_tensor(
                    scm[ms][p0:p0 + cl, :, 0:cl],
                    ps[bsc][p0:p0 + cl, 0:HG * 64].rearrange("p (h t) -> p h t", t=64)[:, :, 0:cl],
                    trim[p0:p0 + cl, 0:cl].unsqueeze(1).to_broadcast([cl, HG, cl]), ALU.mult),
                    reads=[("ps", bsc), K("trim")], writes=[K("scm", ms)])
                yield
                bo = rot_o.next()

                def ofn(e, bo=bo, ms=ms, p0=p0, cs=cs, blk=blk):
                    ins = None
                    for hh in range(HG):
                        o_ap = ps[bo][:, hh * 64:hh * 64 + cl]
                        e.matmul(o_ap, vt[up][p0:p0 + cl, blk, hh, :], scm[ms][p0:p0 + cl, hh, 0:cl], start=True, stop=False)
                        ins = e.matmul(o_ap, Sb[:, hs[hh], :], qt[up][:, hh, cs], start=False, stop=True)
                    return ins
                P.op("pe", ofn, reads=[K("vt", up), K("scm", ms), K("qt", up)] + [K("Sb", h) for h in hs], writes=[("ps", bo)])
                P.op("act", lambda e, bo=bo, cs=cs: e.activation(osb[:, :, cs], ps[bo][:, 0:HG * 64].rearrange("p (h t) -> p h t", t=64)[:, :, 0:cl], AF.Copy),
                     reads=[("ps", bo)], writes=[K("osb")])

            def sfn(e, p0=p0, blk=blk):
                ins = None
                for hh in range(HG):
                    ins = e.matmul(ps[7][:, hh * 128:(hh + 1) * 128], kh[up][p0:p0 + cl, blk, hh, :], vt[up][p0:p0 + cl, blk, hh, :], start=True, stop=True)
                return ins
            P.op("pe", sfn, reads=[K("kh", up), K("vt", up)], writes=[("ps", 7)])
            for hh in range(HG):
                h = hs[hh]
                P.op("dve", lambda e, h=h, hh=hh, c=c: e.scalar_tensor_tensor(S[:, h, :], S[:, h, :], dec[up][:, hh, c:c + 1],
                                                                             ps[7][:, hh * 128:(hh + 1) * 128], ALU.mult, ALU.add),
                     reads=[("ps", 7), K("S", h), K("dec", up)], writes=[K("S", h)])
            P.op("act", lambda e: e.activation(Sb[:, hs[0]:hs[0] + HG, :], S[:, hs[0]:hs[0] + HG, :], AF.Copy),
                 reads=[K("S", h) for h in hs], writes=[K("Sb", h) for h in hs])
            yield
        if nt == 16:
            return
        mp = rot_m.next()
        hps = [[(2 * pair + x, hs[2 * pair + x], x) for x in range(2)] for pair in range(2)]
        banks = {}

        def c_square(hp):
            for hh, h, st in hp:
                P.op("act", lambda e, hh=hh, st=st: e.activation(osq[st][:, :], osb[:, hh, :], AF.Square), reads=[K("osb")], writes=[K("osq", st)])

        def c_norm(hp):
            for hh, h, st in hp:
                bank = rot_proj.next()
                banks[st] = bank
                P.op("pe", lambda e, st=st, bank=bank: e.matmul(ps[bank][:, :], ones[:, :], osq[st][:, :], start=True, stop=True),
                     reads=[K("ones"), K("osq", st)], writes=[("ps", bank)])
            for hh, h, st in hp:
                P.op("act", lambda e, st=st, bank=banks[st]: e.activation(rsd[st][:, :], ps[bank][:, :], AF.Ln, bias=128.0 * EPS),
                     reads=[("ps", banks[st])], writes=[K("rsd", st)])
            for hh, h, st in hp:
                P.op("act", lambda e, st=st: e.activation(rsd[st][:, :], rsd[st][:, :], AF.Exp, scale=-0.5),
                     reads=[K("rsd", st)], writes=[K("rsd", st)])

        def c_out(hp):
            for hh, h, st in hp:
                P.op("dve", lambda e, hh=hh, h=h, st=st: e.scalar_tensor_tensor(rsd[st][:, :], osb[:, hh, :], onw[:, h:h + 1], rsd[st][:, :], ALU.mult, ALU.mult),
                     reads=[K("osb"), K("onw"), K("rsd", st)], writes=[K("rsd", st)])
            for hh, h, st in hp:
                P.op("dve", lambda e, hh=hh, st=st: e.tensor_tensor(mo[mp][:, hh, :], rsd[st][:, :], sgt[up][:, hh, :], ALU.mult),
                     reads=[K("rsd", st), K("sgt", up)], writes=[K("mo", mp)])
        c_square(hps[0])
        yield
        c_norm(hps[0])
        yield
        c_out(hps[0])
        c_square(hps[1])
        yield
        c_norm(hps[1])
        yield
        c_out(hps[1])
        P.op("sp", lambda e: e.dma_start(out=a["m_dst"](ti, hg), in_=mo[mp][:, :, :]),
             reads=[K("mo", mp)], writes=[a["m_key"](ti)], dma_slot=pref + f"om{mp}")
        if after_unit is not None:
            after_unit(ti, hg)
        yield

    units = [(ti, hg) for ti in range(len(tiles)) for hg in range(2)]
    load_dma(0)
    for b in range(nblocks(0)):
        load_T_block(0, b)
    if len(tiles) > 1:
        load_dma(1)
    prevBC = None
    for n, (ti, hg) in enumerate(units):
        gA = stageA(ti, hg, n % 2)
        gB = prevBC
        aliveA, aliveB = True, gB is not None

        def step(g):
            try:
                next(g)
                return True
            except StopIteration:
                return False
        if aliveB:
            aliveA = step(gA)
        while aliveA or aliveB:
            if aliveB:
                aliveB = step(gB)
            if aliveA:
                aliveA = step(gA)
            if aliveB:
                aliveB = step(gB)
        prevBC = stageBC(ti, hg, n % 2)
    for _ in prevBC:
        pass


def build_p3(nc, P, ps, a, ntiles, T=512, pref="p3"):
    sb = SB(nc, pref)
    K = lambda *k: (pref,) + k
    NB = T // 128
    WOK = [K("W_out", k) for k in range(16)]
    W_out = sb.t("W_out", [128, 16, 1024], BF16)
    wtf = sb.t("wtf", [128, D], F32)
    mt = [sb.t(f"mt{i}", [128, 16, T], BF16) for i in range(2)]
    ht = [sb.t(f"ht{i}", [128, NB, D], F32) for i in range(2)]
    ot = [sb.t(f"ot{i}", [128, NB, D], F32) for i in range(2)]
    junk = sb.t("junk", [128, D], BF16)
    ss = sb.t("ss", [128, 8], F32)
    rs = sb.t("rs", [128, 8], F32)
    if a.get("wob") is not None:
        for k in range(16):
            P.op("sp" if k % 2 == 0 else "act", lambda e, k=k: e.dma_start(out=W_out[:, k, :], in_=a["wob"][k * 128:(k + 1) * 128, :]),
                 reads=[("d_wob", 0)], writes=[K("W_out", k)], dma_slot=pref + f"w{k % 2}")
    else:
        P.op("pool", lambda e: [e.dma_start(out=W_out[:, k, :], in_=a["w_out"][k * 128:(k + 1) * 128, :]) for k in range(16)],
             writes=WOK, dma_slot=pref + "w0", npieces=16)
    P.op("sp", lambda e: e.dma_start(out=wtf[:], in_=a["nwf"].partition_broadcast(128)), writes=[K("wtf")], dma_slot=pref + "c0")
    P.op("dve", lambda e: e.tensor_scalar(wtf[:], wtf[:], float(np.sqrt(float(D))), None, ALU.mult), reads=[K("wtf")], writes=[K("wtf")])
    rot_out = Rot([0, 1, 2, 3])

    def load(i):
        s = i % 2
        P.op("sp", lambda e: [e.dma_start(out=mt[s][:, k0:k1, :], in_=src) for (k0, k1, src) in a["m_src"](e, i)],
             reads=list(a["m_keys"](i)), writes=[K("mt", s)], dma_slot=pref + f"m{s}", npieces=a["m_npieces"])
        P.op("sp", lambda e: e.dma_start(out=ht[s][:, :, :], in_=a["h1"][16 + i * T:16 + (i + 1) * T, :].rearrange("(b p) d -> p b d", p=128)),
             writes=[K("ht", s)], dma_slot=pref + f"h{s}")

    def compute(i):
        s = i % 2
        for b in range(NB):
            for hf in range(2):
                bank = rot_out.next()
                mm_group(P, ps[bank][:, :], [(mt[s][:, k, b * 128:(b + 1) * 128], W_out[:, k, hf * 512:(hf + 1) * 512]) for k in range(16)],
                         reads=[K("mt", s)] + WOK, writes=[("ps", bank)])
                P.op("dve", lambda e, b=b, hf=hf, bank=bank: e.tensor_tensor(ht[s][:, b, hf * 512:(hf + 1) * 512],
                                                                           ht[s][:, b, hf * 512:(hf + 1) * 512], ps[bank][:, :], ALU.add),
                     reads=[("ps", bank), K("ht", s)], writes=[K("ht", s)])
            P.op("act", lambda e, b=b: e.activation(junk[:, :], ht[s][:, b, :], AF.Square, accum_out=ss[:, b:b + 1]),
                 reads=[K("ht", s)], writes=[K("junk"), K("ss")])
        rms_rstd(P, ss[:, 0:NB], rs[:, 0:NB], D, [K("ss")], [K("rs")])
        for b in range(NB):
            P.op("dve", lambda e, b=b: e.scalar_tensor_tensor(ot[s][:, b, :], ht[s][:, b, :], rs[:, b:b + 1], wtf[:, :], ALU.mult, ALU.mult),
                 reads=[K("ht", s), K("rs"), K("wtf")], writes=[K("ot", s)])
        P.op("sp", lambda e: e.dma_start(out=a["out"][i * T:(i + 1) * T, :].rearrange("(b p) d -> p b d", p=128), in_=ot[s][:, :, :]),
             reads=[K("ot", s)], dma_slot=pref + f"o{s}")

    load(0)
    for i in range(ntiles):
        if i + 1 < ntiles:
            load(i + 1)
        compute(i)


def _psum(nc, es):
    return [es.enter_context(nc.psum_tensor(f"psb{i}", [128, 512], F32)) for i in range(8)]


def make_p1(ntiles=16, T=256):
    nc = bass.Bass("TRN2", target_bir_lowering=False, dynamic_dma_scratch_size=SCRATCH)
    dt = lambda n, s, d, k: nc.dram_tensor(n, s, d, kind=k).ap()
    a = dict(
        xin=dt("xin", [LTOK, D], F32, "ExternalInput"),
        nw0=dt("nw0", [D], F32, "ExternalInput"), nw1=dt("nw1", [D], F32, "ExternalInput"),
        w_in=dt("w_in", [D, 4096], F32, "ExternalInput"), w_grp=dt("w_grp", [4, 512, 512], F32, "ExternalInput"),
        scale=dt("scale", [2048], F32, "ExternalInput"), w_out=dt("w_out", [2048, D], F32, "ExternalInput"),
        h1=dt("h1", [LTOK, D], F32, "ExternalOutput"), hn1=dt("hn1", [LTOK, D], BF16, "ExternalOutput"),
    )
    with ExitStack() as es:
        ps = _psum(nc, es)
        P = Prog(nc)
        build_p1(nc, P, ps, a, ntiles, T)
        P.finish(es)
    return nc


def make_p2(ntiles=16, T=512):
    nc = bass.Bass("TRN2", target_bir_lowering=False, dynamic_dma_scratch_size=SCRATCH)
    dt = lambda n, s, d, k: nc.dram_tensor(n, s, d, kind=k).ap()
    a = dict(
        hn1=dt("hn1", [LFULL, D], BF16, "ExternalInput"),
        w2=dt("w2", [D, 4096], F32, "ExternalInput"),
        lbl=dt("lbl", [2, 1024], F32, "ExternalInput"),
        onorm=dt("onorm", [1024], F32, "ExternalInput"),
        m=dt("m", [8, 128, SEQ], BF16, "ExternalOutput"),
    )
    a["hn_src"] = lambda ti: a["hn1"][0:16, :] if ti == 0 else a["hn1"][16 + T * (ti - 1):16 + T * ti, :]
    a["hn_key"] = lambda ti: ("d_hn1all", 0)
    a["m_dst"] = lambda ti, hg: a["m"][hg * 4:(hg + 1) * 4, :, T * (ti - 1):T * ti].rearrange("h p t -> p h t")
    a["m_key"] = lambda ti: ("d_mloc", 0)
    with ExitStack() as es:
        ps = _psum(nc, es)
        P = Prog(nc)
        build_p2(nc, P, ps, a, ntiles, T)
        P.finish(es)
    return nc


def make_p3(ntiles=8, T=512):
    nc = bass.Bass("TRN2", target_bir_lowering=False, dynamic_dma_scratch_size=SCRATCH)
    dt = lambda n, s, d, k: nc.dram_tensor(n, s, d, kind=k).ap()
    a = dict(
        m=dt("m", [16, 128, HALF], BF16, "ExternalInput"),
        h1=dt("h1", [LTOK, D], F32, "ExternalInput"),
        w_out=dt("w_out", [2048, D], F32, "ExternalInput"),
        nwf=dt("nwf", [D], F32, "ExternalInput"),
        out=dt("out", [HALF, D], F32, "ExternalOutput"),
    )
    a["m_src"] = lambda e, i: [(0, 16, a["m"][:, :, i * T:(i + 1) * T].rearrange("h p t -> p h t"))]
    a["m_npieces"] = 1
    a["m_keys"] = lambda i: [("d_mall", 0)]
    with ExitStack() as es:
        ps = _psum(nc, es)
        P = Prog(nc)
        build_p3(nc, P, ps, a, ntiles, T)
        P.finish(es)
    return nc


GROUPS = [[0, 1], [2, 3], [4, 5], [6, 7]]


def make_fused(nt1=16, nt2=16, nt3=8, T1=256, T2=512, T3=512):
    nc = bass.Bass("TRN2", target_bir_lowering=False, dynamic_dma_scratch_size=SCRATCH)
    dt = lambda n, s, d, k="Internal": nc.dram_tensor(n, s, d, kind=k).ap()
    EI, EO = "ExternalInput", "ExternalOutput"
    rows = [1024, 1024, 1024, 1024]
    row0 = [NMETA, NMETA + 1024, NMETA + 2048, NMETA + 3072]
    hn1meta = dt("hn1meta", [2 * NMETA, D], BF16)
    h1 = dt("h1", [LTOK, D], F32)
    hn1loc = dt("hn1loc", [LTOK, D], BF16)
    hn1all = [dt(f"hn1all{j}", [2 * rows[j], D], BF16) for j in range(4)]
    mloc = dt("mloc", [2, 4, 128, 2, 8, T2], BF16)
    mall = dt("mall", [2, 4, 2, 128, 2, 8, T2], BF16)
    a1 = dict(
        xin=dt("xin", [LTOK, D], F32, EI), nw0=dt("nw0", [D], F32, EI), nw1=dt("nw1", [D], F32, EI),
        w_in=dt("w_in", [D, 4096], F32, EI), w_grp=dt("w_grp", [4, 512, 512], F32, EI),
        scale=dt("scale", [2048], F32, EI), w_out=dt("w_out0", [2048, D], F32, EI),
        h1=h1, hn1=hn1loc)
    a2 = dict(w2=dt("w2", [D, 4096], F32, EI), lbl=dt("lbl", [2, 1024], F32, EI), onorm=dt("onorm", [1024], F32, EI))
    a3 = dict(h1=h1, w_out=dt("w_out1", [2048, D], F32, EI), nwf=dt("nwf", [D], F32, EI), out=dt("out", [HALF, D], F32, EO))
    a2["w2b"] = dt("w2b", [D, 4096], BF16)
    a3["wob"] = dt("wob", [2048, D], BF16)

    def hn_loc(ti):
        if ti == 0:
            return "meta", 0
        i = ti - 1
        rank, li = i // 8, T2 * (i % 8)
        j = li // 1024
        return j, rank * rows[j] + li - 1024 * j
    def hn_src(ti):
        j, r = hn_loc(ti)
        if ti == 0:
            return hn1meta[0:NMETA, :]
        return hn1all[j][r:r + T2, :]
    a2["hn_src"] = hn_src
    a2["hn_key"] = lambda ti: ("d_hn1all", hn_loc(ti)[0])
    def m_piece(ti):
        i = ti - 1
        return i // 8, (i % 8) // 2, i % 2
    def m_dst(ti, hg):
        half, q, w = m_piece(ti)
        return mloc[half, q, :, w, hg * 4:(hg + 1) * 4, :]
    a2["m_dst"] = m_dst
    a2["m_key"] = lambda ti: ("d_mloc", m_piece(ti)[0] * 4 + m_piece(ti)[1])
    par_cache = {}

    def m_src(e, i):
        if "p" not in par_cache:
            par_cache["p"] = e.snap(e.partition_id() % 2)
        par = par_cache["p"]
        q, w = i // 2, i % 2
        return [(8 * r, 8 * r + 8, mall[bass.ds(par, 1), q, r, :, w, :, :].rearrange("a p h t -> p (a h) t")) for r in range(2)]
    a3["m_src"] = m_src
    a3["m_npieces"] = 2
    a3["m_keys"] = lambda i: [("d_mall", 0, i // 2), ("d_mall", 1, i // 2)]

    with ExitStack() as es:
        ps = _psum(nc, es)
        P = Prog(nc)
        pend = []

        def flush():
            while pend:
                pend.pop(0)()

        def gather_h(j):
            P.op("pool", lambda e: e.collective_compute("AllGather", ALU.bypass, replica_groups=GROUPS,
                                                        ins=[hn1loc[row0[j]:row0[j] + rows[j], :]], outs=[hn1all[j]]),
                 reads=[("d_hn1loc", j)], writes=[("d_hn1all", j)], dma_slot=f"cch{j}", inc=1, exempt=True)

        def gather_meta():
            P.op("pool", lambda e: e.collective_compute("AllGather", ALU.bypass, replica_groups=GROUPS,
                                                        ins=[hn1loc[0:NMETA, :]], outs=[hn1meta]),
                 reads=[("d_hn1loc", "meta")], writes=[("d_hn1all", "meta")], dma_slot="cchm", inc=1, exempt=True)

        def after_tile(ti):
            flush()
            if 3 <= ti < 15:
                precast_piece(ti - 3)
            if ti == 0:
                pend.append(gather_meta)
            if ti >= 1 and ti % 4 == 0:
                j = ti // 4 - 1
                pend.append(lambda j=j: gather_h(j))
        def precast_piece(n):
            if n < 8:
                P.op("pool", lambda e: e.dma_start(out=a2["w2b"][n * 128:(n + 1) * 128, :], in_=a2["w2"][n * 128:(n + 1) * 128, :]),
                     writes=[("d_w2b", 0)], dma_slot="pc2", exempt=True)
            elif n < 12:
                k = n - 8
                P.op("pool", lambda e: e.dma_start(out=a3["wob"][k * 512:(k + 1) * 512, :], in_=a3["w_out"][k * 512:(k + 1) * 512, :]),
                     writes=[("d_wob", 0)], dma_slot="pc3", exempt=True)
        import os
        PH = os.environ.get("FUSE_PHASES", "123")
        NOG = os.environ.get("FUSE_NOGATHER", "0") == "1"
        if NOG:
            gather_h = lambda j: None
            gather_meta = lambda: None
        if "1" in PH:
            build_p1(nc, P, ps, a1, nt1, T1, after_tile=after_tile)
        flush()
        P.barrier()

        def gather_m(half, q):
            P.op("pool", lambda e: e.collective_compute("AllGather", ALU.bypass, replica_groups=GROUPS,
                                                        ins=[mloc[half, q].rearrange("p w h t -> p (w h t)")],
                                                        outs=[mall[half, q].rearrange("r p w h t -> (r p) (w h t)")]),
                 reads=[("d_mloc", half * 4 + q)], writes=[("d_mall", half, q)], dma_slot=f"ccm{half}{q}", inc=1, exempt=True)

        def after_unit(ti, hg):
            flush()
            i = ti - 1
            if hg == 1 and i % 2 == 1:
                half, q, _ = m_piece(ti)
                pend.append(lambda half=half, q=q: gather_m(half, q))
        if NOG:
            gather_m = lambda half, q: None
        if "2" in PH:
            build_p2(nc, P, ps, a2, nt2, T2, after_unit=after_unit)
        flush()
        P.barrier()
        if "3" in PH:
            build_p3(nc, P, ps, a3, nt3, T3)
        P.finish(es)
    return nc


def fused_inputs(x, meta_tokens, norm_w, pool_w_in, pool_w_grp, pool_scale, pool_w_out,
                 hgrn_w_in, hgrn_lb_logits, hgrn_o_norm, hgrn_w_out, final_norm_w):
    m1 = p1_inputs(x, meta_tokens, norm_w, pool_w_in, pool_w_grp, pool_scale, pool_w_out)
    maps = []
    for c in range(8):
        d = dict(m1[c])
        d["w_out0"] = d.pop("w_out")
        d.update(p2_weights(c, hgrn_w_in, hgrn_lb_logits, hgrn_o_norm))
        d["w_out1"] = hgrn_w_out[0]
        d["nwf"] = final_norm_w
        maps.append(d)
    return maps


def p1_inputs(x, meta_tokens, norm_w, pool_w_in, pool_w_grp, pool_scale, pool_w_out):
    maps = []
    for c in range(8):
        b, half = c // 2, c % 2
        if half == 0:
            xin = np.concatenate([meta_tokens, x[b, 0:HALF]], axis=0)
        else:
            xin = x[b, HALF - NMETA:SEQ]
        maps.append(dict(xin=np.ascontiguousarray(xin), nw0=norm_w[0], nw1=norm_w[1], w_in=pool_w_in[0], w_grp=pool_w_grp[0],
                         scale=pool_scale[0], w_out=pool_w_out[0]))
    return maps


def p2_weights(c, hgrn_w_in, hgrn_lb_logits, hgrn_o_norm):
    hg = c % 2
    w = hgrn_w_in[0]
    cols = [w[:, part * 2048 + hg * 1024: part * 2048 + (hg + 1) * 1024] for part in range(4)]
    return dict(w2=np.ascontiguousarray(np.concatenate(cols, axis=1)),
                lbl=np.ascontiguousarray(hgrn_lb_logits[:, hg * 1024:(hg + 1) * 1024]),
                onorm=np.ascontiguousarray(hgrn_o_norm[0, hg * 1024:(hg + 1) * 1024]))


def kernel_unfused(x, meta_tokens, norm_w, pool_w_in, pool_w_grp, pool_scale, pool_w_out,
                   hgrn_w_in, hgrn_lb_logits, hgrn_o_norm, hgrn_w_out, final_norm_w):
    f = lambda t: np.asarray(t, dtype=np.float32)
    x, meta_tokens, norm_w = f(x), f(meta_tokens), f(norm_w)
    cores = list(range(8))
    nc1 = make_p1()
    r1 = run_bass_kernel_spmd(nc1, p1_inputs(x, meta_tokens, norm_w, f(pool_w_in), f(pool_w_grp), f(pool_scale), f(pool_w_out)),
                              core_ids=cores).results
    nc2 = make_p2()
    maps2 = []
    for c in cores:
        b = c // 2
        hn_full = np.concatenate([r1[2 * b]["hn1"], r1[2 * b + 1]["hn1"][NMETA:]], axis=0)
        d = p2_weights(c, f(hgrn_w_in), f(hgrn_lb_logits), f(hgrn_o_norm))
        d["hn1"] = np.ascontiguousarray(hn_full)
        maps2.append(d)
    r2 = run_bass_kernel_spmd(nc2, maps2, core_ids=cores).results
    nc3 = make_p3()
    maps3 = []
    for c in cores:
        b, half = c // 2, c % 2
        m = np.concatenate([r2[2 * b]["m"][:, :, half * HALF:(half + 1) * HALF],
                            r2[2 * b + 1]["m"][:, :, half * HALF:(half + 1) * HALF]], axis=0)
        maps3.append(dict(m=np.ascontiguousarray(m), h1=r1[c]["h1"], w_out=f(hgrn_w_out)[0], nwf=f(final_norm_w)))
    r3 = run_bass_kernel_spmd(nc3, maps3, core_ids=cores).results
    out = np.empty((4, SEQ, D), np.float32)
    for c in cores:
        b, half = c // 2, c % 2
        out[b, half * HALF:(half + 1) * HALF] = r3[c]["out"]
    return out


def kernel(x, meta_tokens, norm_w, pool_w_in, pool_w_grp, pool_scale, pool_w_out,
           hgrn_w_in, hgrn_lb_logits, hgrn_o_norm, hgrn_w_out, final_norm_w):
    f = lambda t: np.ascontiguousarray(np.asarray(t, dtype=np.float32))
    maps = fused_inputs(f(x), f(meta_tokens), f(norm_w), f(pool_w_in), f(pool_w_grp), f(pool_scale), f(pool_w_out),
                        f(hgrn_w_in), f(hgrn_lb_logits), f(hgrn_o_norm), f(hgrn_w_out), f(final_norm_w))
    nc = make_fused()
    res = run_bass_kernel_spmd(nc, maps, core_ids=list(range(8))).results
    out = np.empty((4, SEQ, D), np.float32)
    for c in range(8):
        b, half = c // 2, c % 2
        out[b, half * HALF:(half + 1) * HALF] = res[c]["out"]
    return out
```
